# Optimizing a Trainium2 kernel written in Bass

```python
import math
import jax, jax.numpy as jnp
from jax import lax
import numpy as np

D_MODEL = 4096
BATCH = 4
SEQ = 4096
DEPTH = 4

CHUNK = 64
N_MIXERS = 2
SSD_EXPAND = 2
SSD_D_INNER = SSD_EXPAND * D_MODEL
SSD_HEADDIM = 64
SSD_HEADS = SSD_D_INNER // SSD_HEADDIM
SSD_GROUPS = 8
SSD_HEADS_PER_GROUP = SSD_HEADS // SSD_GROUPS
SSD_STATE = 128
SSD_CONV = 4
SSD_CONV_DIM = SSD_D_INNER + 2 * SSD_GROUPS * SSD_STATE
SSD_PROJ = SSD_D_INNER + SSD_CONV_DIM + SSD_HEADS
SC_WIDTH = 3
PEER_HEADS = 8
PEER_NKEYS = 128
PEER_EXPERTS = PEER_NKEYS * PEER_NKEYS
PEER_TOPK = 16
PEER_HALF_DIM = 128
PEER_QUERY_DIM = 2 * PEER_HALF_DIM
PEER_TOKEN_BLOCK = 128
LN_EPS = 1e-5
RMS_EPS = 1e-5
DEEPNORM_ALPHA = (2 * DEPTH) ** 0.25
DEEPNORM_BETA = (8 * DEPTH) ** -0.25

kernel_name = "hybrid_ssd_shortconv_peer_deepnorm"


def layer_norm(h, g, b):
    hf = h.astype(jnp.float32)
    mu = jnp.mean(hf, axis=-1, keepdims=True)
    var = jnp.mean(jnp.square(hf - mu), axis=-1, keepdims=True)
    out = (hf - mu) * lax.rsqrt(var + LN_EPS) * g.astype(jnp.float32) + b.astype(jnp.float32)
    return out.astype(h.dtype)


def causal_dwconv(h, w):
    k_width = w.shape[0]
    s = h.shape[1]
    hp = jnp.pad(h, ((0, 0), (k_width - 1, 0), (0, 0)))
    out = hp[:, 0:s] * w[0]
    for k in range(1, k_width):
        out = out + hp[:, k:k + s] * w[k]
    return out


def ssd_chunked_scan(xs, dt, a, bm, cm):
    bn, s, g, j, p = xs.shape
    n = bm.shape[-1]
    nc = s // CHUNK
    f32 = jnp.float32
    xdt = (xs.astype(f32) * dt[..., None]).reshape(bn, nc, CHUNK, g, j, p)
    bc = bm.astype(f32).reshape(bn, nc, CHUNK, g, n)
    cc = cm.astype(f32).reshape(bn, nc, CHUNK, g, n)
    acs = jnp.cumsum((dt * a).reshape(bn, nc, CHUNK, g, j), axis=2)
    acs_t = jnp.moveaxis(acs, 2, -1)
    seg = acs_t[..., :, None] - acs_t[..., None, :]
    causal = jnp.tril(jnp.ones((CHUNK, CHUNK), dtype=bool))
    decay_ls = jnp.exp(jnp.where(causal, seg, -jnp.inf))
    cb = jnp.einsum('bclgn,bcsgn->bcgls', cc, bc)
    w_ls = cb[:, :, :, None] * decay_ls
    y_diag = jnp.einsum('bcgjls,bcsgjp->bclgjp', w_ls, xdt)

    def step(state, inp):
        b_k, c_k, xdt_k, acs_k = inp
        y_off = jnp.einsum('blgn,bgjpn->blgjp', c_k, state) * jnp.exp(acs_k)[..., None]
        to_end = jnp.exp(acs_k[:, -1:] - acs_k)
        new_state = state * jnp.exp(acs_k[:, -1])[..., None, None] + jnp.einsum(
            'blgn,blgjp->bgjpn', b_k, xdt_k * to_end[..., None])
        return new_state, y_off

    init = jnp.zeros((bn, g, j, p, n), f32)
    swap = lambda t: jnp.moveaxis(t, 1, 0)
    _, y_off = lax.scan(step, init, (swap(bc), swap(cc), swap(xdt), swap(acs)))
    y = y_diag + jnp.moveaxis(y_off, 0, 1)
    return y.reshape(bn, s, g, j, p)


def ssd_mixer(x, in_proj, conv_w, conv_b, dt_bias, a_log, d_skip, norm_w, out_proj):
    bn, s, _ = x.shape
    f32 = jnp.float32
    zxbcdt = x @ in_proj
    z = zxbcdt[..., :SSD_D_INNER]
    xbc = zxbcdt[..., SSD_D_INNER:SSD_D_INNER + SSD_CONV_DIM]
    dt_raw = zxbcdt[..., SSD_D_INNER + SSD_CONV_DIM:]
    xbc = jax.nn.silu(causal_dwconv(xbc, conv_w) + conv_b)
    gn = SSD_GROUPS * SSD_STATE
    xs = xbc[..., :SSD_D_INNER].reshape(bn, s, SSD_GROUPS, SSD_HEADS_PER_GROUP, SSD_HEADDIM)
    bm = xbc[..., SSD_D_INNER:SSD_D_INNER + gn].reshape(bn, s, SSD_GROUPS, SSD_STATE)
    cm = xbc[..., SSD_D_INNER + gn:].reshape(bn, s, SSD_GROUPS, SSD_STATE)
    dt = jax.nn.softplus(dt_raw.astype(f32) + dt_bias.astype(f32))
    dt = dt.reshape(bn, s, SSD_GROUPS, SSD_HEADS_PER_GROUP)
    a = -jnp.exp(a_log.astype(f32)).reshape(SSD_GROUPS, SSD_HEADS_PER_GROUP)
    y = ssd_chunked_scan(xs, dt, a, bm, cm)
    y = y + d_skip.astype(f32).reshape(SSD_GROUPS, SSD_HEADS_PER_GROUP, 1) * xs.astype(f32)
    h = (y.reshape(bn, s, SSD_D_INNER) * jax.nn.silu(z.astype(f32))).reshape(bn, s, SSD_GROUPS, -1)
    h = h * lax.rsqrt(jnp.mean(jnp.square(h), axis=-1, keepdims=True) + RMS_EPS)
    h = h.reshape(bn, s, SSD_D_INNER) * norm_w.astype(f32)
    return h.astype(x.dtype) @ out_proj


def shortconv_mixer(x, in_proj, conv_w, out_proj):
    bch = x @ in_proj
    gate_b = bch[..., :D_MODEL]
    gate_c = bch[..., D_MODEL:2 * D_MODEL]
    h = bch[..., 2 * D_MODEL:]
    y = gate_b * causal_dwconv(gate_c * h, conv_w)
    return y @ out_proj


def peer_ffn(x, wq, subkeys, u, v):
    bn, s, d = x.shape
    t = bn * s
    xt = x.reshape(t, d)
    q = (xt @ wq).reshape(t, PEER_HEADS, 2, PEER_HALF_DIM).astype(jnp.float32)
    sc = jnp.einsum('thid,hikd->thik', q, subkeys.astype(jnp.float32))
    s1, i1 = lax.top_k(sc[:, :, 0], PEER_TOPK)
    s2, i2 = lax.top_k(sc[:, :, 1], PEER_TOPK)
    cand = (s1[..., :, None] + s2[..., None, :]).reshape(t, PEER_HEADS, PEER_TOPK * PEER_TOPK)
    cidx = (i1[..., :, None] * PEER_NKEYS + i2[..., None, :]).reshape(t, PEER_HEADS, PEER_TOPK * PEER_TOPK)
    top_s, pos = lax.top_k(cand, PEER_TOPK)
    eidx = jnp.take_along_axis(cidx, pos, axis=-1)
    gates = jax.nn.softmax(top_s, axis=-1).astype(x.dtype)

    nb = t // PEER_TOKEN_BLOCK

    def block(args):
        xb, eb, gb = args
        ub = jnp.take(u, eb, axis=0)
        act = jax.nn.gelu(jnp.einsum('td,thkd->thk', xb, ub), approximate=False) * gb
        vb = jnp.take(v, eb, axis=0)
        return jnp.einsum('thk,thkd->td', act, vb)

    out = lax.map(block, (xt.reshape(nb, PEER_TOKEN_BLOCK, d),
                          eidx.reshape(nb, PEER_TOKEN_BLOCK, PEER_HEADS, PEER_TOPK),
                          gates.reshape(nb, PEER_TOKEN_BLOCK, PEER_HEADS, PEER_TOPK)))
    return out.reshape(bn, s, d)


def setup_inputs(seed: int = 0) -> dict:
    key = jax.random.key(seed)
    ks = jax.random.split(key, 24)
    n_ssd = (DEPTH + 1) // 2
    n_sc = DEPTH // 2
    nrm = jax.random.normal
    f32 = jnp.float32
    x = nrm(ks[0], (BATCH, SEQ, D_MODEL), f32)
    ssd_in_proj = nrm(ks[1], (n_ssd, D_MODEL, SSD_PROJ), f32) * D_MODEL ** -0.5
    ssd_conv_w = nrm(ks[2], (n_ssd, SSD_CONV, SSD_CONV_DIM), f32) * SSD_CONV ** -0.5
    ssd_conv_b = nrm(ks[3], (n_ssd, SSD_CONV_DIM), f32) * 0.01
    dt0 = jnp.exp(jax.random.uniform(ks[4], (n_ssd, SSD_HEADS), f32)
                  * (math.log(0.1) - math.log(0.001)) + math.log(0.001))
    ssd_dt_bias = dt0 + jnp.log(-jnp.expm1(-dt0))
    ssd_A_log = jnp.log(jax.random.uniform(ks[5], (n_ssd, SSD_HEADS), f32, minval=1.0, maxval=16.0))
    ssd_D = 1.0 + 0.1 * nrm(ks[6], (n_ssd, SSD_HEADS), f32)
    ssd_norm_w = 1.0 + 0.1 * nrm(ks[7], (n_ssd, SSD_D_INNER), f32)
    ssd_out_proj = nrm(ks[8], (n_ssd, SSD_D_INNER, D_MODEL), f32) * (SSD_D_INNER ** -0.5 * DEEPNORM_BETA)
    sc_in_proj = nrm(ks[9], (n_sc, D_MODEL, 3 * D_MODEL), f32) * D_MODEL ** -0.5
    sc_conv_w = nrm(ks[10], (n_sc, SC_WIDTH, D_MODEL), f32) * SC_WIDTH ** -0.5
    sc_out_proj = nrm(ks[11], (n_sc, D_MODEL, D_MODEL), f32) * (D_MODEL ** -0.5 * DEEPNORM_BETA)
    peer_wq = nrm(ks[12], (DEPTH, D_MODEL, PEER_HEADS * PEER_QUERY_DIM), f32) * D_MODEL ** -0.5
    peer_subkeys = nrm(ks[13], (DEPTH, PEER_HEADS, 2, PEER_NKEYS, PEER_HALF_DIM), f32) * PEER_HALF_DIM ** -0.5
    peer_u = nrm(ks[14], (DEPTH, PEER_EXPERTS, D_MODEL), f32) * D_MODEL ** -0.5
    peer_v = nrm(ks[15], (DEPTH, PEER_EXPERTS, D_MODEL), f32) * (DEEPNORM_BETA * PEER_HEADS ** -0.5)
    ln1_g = 1.0 + 0.1 * nrm(ks[16], (DEPTH, D_MODEL), f32)
    ln1_b = 0.01 * nrm(ks[17], (DEPTH, D_MODEL), f32)
    ln2_g = 1.0 + 0.1 * nrm(ks[18], (DEPTH, D_MODEL), f32)
    ln2_b = 0.01 * nrm(ks[19], (DEPTH, D_MODEL), f32)
    return {'x': x, 'ssd_in_proj': ssd_in_proj, 'ssd_conv_w': ssd_conv_w, 'ssd_conv_b': ssd_conv_b,
            'ssd_dt_bias': ssd_dt_bias, 'ssd_A_log': ssd_A_log, 'ssd_D': ssd_D, 'ssd_norm_w': ssd_norm_w,
            'ssd_out_proj': ssd_out_proj, 'sc_in_proj': sc_in_proj, 'sc_conv_w': sc_conv_w,
            'sc_out_proj': sc_out_proj, 'peer_wq': peer_wq, 'peer_subkeys': peer_subkeys,
            'peer_u': peer_u, 'peer_v': peer_v, 'ln1_g': ln1_g, 'ln1_b': ln1_b,
            'ln2_g': ln2_g, 'ln2_b': ln2_b}


def reference(x, ssd_in_proj, ssd_conv_w, ssd_conv_b, ssd_dt_bias, ssd_A_log, ssd_D, ssd_norm_w,
              ssd_out_proj, sc_in_proj, sc_conv_w, sc_out_proj, peer_wq, peer_subkeys,
              peer_u, peer_v, ln1_g, ln1_b, ln2_g, ln2_b):
    for i in range(DEPTH):
        j = i // N_MIXERS
        if i % N_MIXERS == 0:
            mix = ssd_mixer(x, ssd_in_proj[j], ssd_conv_w[j], ssd_conv_b[j], ssd_dt_bias[j],
                            ssd_A_log[j], ssd_D[j], ssd_norm_w[j], ssd_out_proj[j])
        else:
            mix = shortconv_mixer(x, sc_in_proj[j], sc_conv_w[j], sc_out_proj[j])
        x = layer_norm(DEEPNORM_ALPHA * x + mix, ln1_g[i], ln1_b[i])
        ffn = peer_ffn(x, peer_wq[i], peer_subkeys[i], peer_u[i], peer_v[i])
        x = layer_norm(DEEPNORM_ALPHA * x + ffn, ln2_g[i], ln2_b[i])
    return x
```

```python
import math
from contextlib import ExitStack
import numpy as np
import concourse.bass as bass
import concourse.mybir as mybir
from concourse.bass_utils import run_bass_kernel_spmd

F32 = mybir.dt.float32
BF16 = mybir.dt.bfloat16
AF = mybir.ActivationFunctionType
ALU = mybir.AluOpType
AX = mybir.AxisListType

NCORES = 4
RW = 4096
TB = 256
NDS = 40


class Cfg:
    def __init__(s, D=4096, SEQ=4096, BATCH=4, DEPTH=4, DI=8192, NH=128, G=8, NST=128,
                 PH=8, NK=128, TOPK=16):
        s.D, s.SEQ, s.BATCH, s.DEPTH = D, SEQ, BATCH, DEPTH
        s.DI, s.NH, s.G, s.NST = DI, NH, G, NST
        s.HD = DI // NH
        s.HPG = NH // G
        s.GW = s.HPG * s.HD
        s.CONVD = DI + 2 * G * NST
        s.PH, s.NK, s.TOPK = PH, NK, TOPK
        s.E = NK * NK
        s.QD = PH * 2 * 128
        s.TOK = SEQ
        s.KC = D // 128
        s.alpha = (2 * DEPTH) ** 0.25
        assert s.HD == 64 and NST == 128 and NK == 128 and TOPK == 16


def fm_tiles(W):
    K, F = W.shape
    return np.ascontiguousarray(W.reshape(K // 128, 128, F // 128, 128).transpose(2, 1, 0, 3))


def kgs_of(K):
    return min(16, K // 128)


def tm_tiles(W, FW):
    K, F = W.shape
    kgs = kgs_of(K)
    KG = K // 128 // kgs
    return np.ascontiguousarray(W.reshape(KG, kgs, 128, F // FW, FW).transpose(3, 0, 2, 1, 4))


def shard_blob(a):
    flat = a.reshape(-1)
    n = (flat.size + RW - 1) // RW * RW
    if n != flat.size:
        flat = np.concatenate([flat, np.zeros(n - flat.size, flat.dtype)])
    return flat.reshape(-1, RW)


class Sched:
    def __init__(s, nc, es):
        s.nc = nc
        s.eng = dict(pe=nc.tensor, dve=nc.vector, act=nc.scalar, pool=nc.gpsimd, sp=nc.sync)
        s.semobj = {}
        s.cnt = {}
        for k in ("pe", "dve", "act", "pool"):
            s.semobj[k] = es.enter_context(nc.semaphore("c_" + k))
            s.cnt[k] = 0
        s.dcnt = [0] * NDS
        for i in range(NDS):
            s.semobj[("d", i)] = es.enter_context(nc.semaphore("d%d" % i))
        s.dnext = 0
        s.bar = es.enter_context(nc.semaphore("bar"))
        s.semobj["bar"] = s.bar
        s.nbar = 0
        s.waited = {}
        s.res = {}
        s.nsem = 0
        s.es = es
        s.wcast = {}

    def newsem(s, name):
        sem = s.es.enter_context(s.nc.semaphore(name))
        s.semobj[name] = sem
        return name

    def _wait(s, e, ev):
        if ev is None:
            return
        sid, val = ev
        if sid == e and e == "pe":
            return
        if s.waited.get((e, sid), 0) >= val:
            return
        s.eng[e].wait_ge(s.semobj[sid], val)
        s.waited[(e, sid)] = val

    @staticmethod
    def _is_ps(k):
        return isinstance(k, tuple) and k[0] in ("psA", "psT", "psF")

    def _deps(s, e, reads, writes):
        for r in reads:
            st = s.res.get(r)
            if st:
                if s._is_ps(r):
                    if st[0] is not None and st[0][0] != e:
                        s._wait(e, st[0])
                    for sid, val in st[1].items():
                        if sid != e:
                            s._wait(e, (sid, val))
                else:
                    s._wait(e, st[0])
        for w in writes:
            st = s.res.get(w)
            if st:
                if s._is_ps(w):
                    if st[0] is not None and st[0][0] != e:
                        s._wait(e, st[0])
                    for sid, val in st[1].items():
                        if sid != e:
                            s._wait(e, (sid, val))
                else:
                    s._wait(e, st[0])
                    for sid, val in st[1].items():
                        s._wait(e, (sid, val))

    def _commit(s, ev, reads, writes):
        for r in reads:
            st = s.res.setdefault(r, [None, {}])
            if st[1].get(ev[0], 0) < ev[1]:
                st[1][ev[0]] = ev[1]
        for w in writes:
            s.res[w] = [ev, {}]

    def op(s, e, fn, reads=(), writes=()):
        s._deps(e, reads, writes)
        ins = fn(s.eng[e])
        s.cnt[e] += 1
        ins.then_inc(s.semobj[e], 1)
        s._commit((e, s.cnt[e]), reads, writes)

    def dma(s, out, in_, reads=(), writes=()):
        q = "sp"
        i = s.dnext
        s.dnext = (i + 1) % NDS
        if s.dcnt[i]:
            s._wait(q, (("d", i), 16 * s.dcnt[i]))
        s._deps(q, reads, writes)
        s.eng[q].dma_start(out=out, in_=in_).then_inc(s.semobj[("d", i)], 16)
        s.dcnt[i] += 1
        s._commit((("d", i), 16 * s.dcnt[i]), reads, writes)

    def ext_event(s, res, ev):
        s.res[res] = [ev, {}]

    def barrier(s, scratch_a, scratch_b):
        for k in ("pe", "dve", "act", "pool"):
            if s.cnt[k]:
                s._wait("sp", (k, s.cnt[k]))
        for i in range(NDS):
            if s.dcnt[i]:
                s._wait("sp", (("d", i), 16 * s.dcnt[i]))
        s.nbar += 1
        s.eng["sp"].dma_start(out=scratch_a, in_=scratch_b).then_inc(s.bar, 16)
        for k in ("pe", "dve", "act", "pool", "sp"):
            s._wait(k, ("bar", 16 * s.nbar))
        s.res = {}


def build_program(cfg):
    c = cfg
    D, TOK, KC, DI, NH, G, GW, HPG, CONVD = c.D, c.TOK, c.KC, c.DI, c.NH, c.G, c.GW, c.HPG, c.CONVD
    NB = TOK // TB
    NTT = TB // 128
    n_ssd = (c.DEPTH + 1) // 2
    n_sc = c.DEPTH // 2
    nc = bass.Bass("TRN2", target_bir_lowering=False)
    es = ExitStack()
    S = Sched(nc, es)

    def dram_in(name, shape, dt=F32):
        return nc.dram_tensor(name, list(shape), dt, kind="ExternalInput").ap()

    x_in = dram_in("x", [TOK, D])
    consts_in = dram_in("consts", [128, 4 * 128])
    y_out = nc.dram_tensor("y", [TOK, D], F32, kind="ExternalOutput").ap()

    blob_shapes = {}

    def blob_rows(nelem):
        return (nelem + RW - 1) // RW

    wspecs = []
    for j in range(n_ssd):
        wspecs += [("ssd_wz%d" % j, D * DI), ("ssd_wx%d" % j, D * CONVD), ("ssd_wdt%d" % j, D * NH),
                   ("ssd_wo%d" % j, DI * D)]
    for j in range(n_sc):
        wspecs += [("sc_wi%d" % j, D * 3 * D), ("sc_wo%d" % j, D * D)]
    for i in range(c.DEPTH):
        wspecs += [("wq%d" % i, D * c.QD), ("ut%d" % i, D * c.E), ("v%d" % i, c.E * D)]
    win, wbounce, wg = {}, {}, {}
    for name, nelem in wspecs:
        r = blob_rows(nelem)
        win[name] = dram_in(name, [r, RW])
        wg[name] = nc.dram_tensor("wg_" + name, [r, RW], BF16)

    sk_in = dram_in("skT", [c.DEPTH, 128, 16 * 128])
    ln_in = dram_in("lnp", [c.DEPTH, 4, D])
    if n_ssd:
        ssd_cw_in = dram_in("ssd_cw", [n_ssd, 128, (CONVD // 128) * 5])
        ssd_hp_in = dram_in("ssd_hp", [n_ssd, 3, NH])
        ssd_nw_in = dram_in("ssd_nw", [n_ssd, DI])
    if n_sc:
        sc_cw_in = dram_in("sc_cw", [n_sc, 128, KC * 3])

    XR = nc.dram_tensor("XR", [TOK, D], F32).ap()
    XT = nc.dram_tensor("XT", [128, KC, TOK], BF16).ap()
    DELTA = nc.dram_tensor("DELTA", [TOK, D], F32).ap()
    if n_ssd:
        XS = nc.dram_tensor("XS", [TOK, DI], F32).ap()
        ZS = nc.dram_tensor("ZS", [TOK, DI], F32).ap()
        BTd = nc.dram_tensor("BTd", [G, 128, TOK], BF16).ap()
        CTd = nc.dram_tensor("CTd", [G, 128, TOK], BF16).ap()
        BTM = nc.dram_tensor("BTM", [TOK, G * 128], BF16).ap()
        DTd = nc.dram_tensor("DTd", [TOK, NH], F32).ap()
        DAd = nc.dram_tensor("DAd", [TOK, NH], F32).ap()
        HT = nc.dram_tensor("HT", [128, DI // 128, TOK], BF16).ap()
    BAR = nc.dram_tensor("BARS", [2, 64], F32).ap()

    psA = [es.enter_context(nc.psum_tensor("psA%d" % i, [128, 512], F32)) for i in range(6)]
    psT = es.enter_context(nc.psum_tensor("psT", [128, 1024], BF16))
    psF = es.enter_context(nc.psum_tensor("psF", [128, 512], F32))

    uid = [0]

    def sb(name, shape, dt=F32, stack=None):
        uid[0] += 1
        return (stack or es).enter_context(nc.sbuf_tensor("%s_%d" % (name, uid[0]), list(shape), dt))

    cst = sb("cst", [128, 512])
    ident = cst[:, 0:128]
    tri = cst[:, 128:256]
    Umat = cst[:, 256:384]
    ones = cst[:, 384:512]
    identb = sb("identb", [128, 128], BF16)
    one_col = sb("one_col", [128, 1])
    eps_col = sb("eps_col", [128, 1])

    S.dma(cst[:], consts_in, writes=["cst"])
    S.op("dve", lambda e: e.tensor_copy(out=identb[:], in_=ident), reads=["cst"], writes=["identb"])
    S.op("dve", lambda e: e.memset(one_col[:], 1.0), writes=["one_col"])
    S.op("dve", lambda e: e.memset(eps_col[:], 1e-5), writes=["eps_col"])

    g = nc.gpsimd
    CH = 1024
    import os as _os
    for name, nelem in wspecs:
        r = win[name].shape[0]
        S.newsem("ag_" + name)
        S.wcast[name] = 16 * ((r + CH - 1) // CH)

    def layer_weights(layer):
        j = layer // 2
        if layer % 2 == 0:
            names = ["ssd_wx%d" % j, "ssd_wz%d" % j, "ssd_wdt%d" % j, "ssd_wo%d" % j]
        else:
            names = ["sc_wi%d" % j, "sc_wo%d" % j]
        return names + ["wq%d" % layer, "ut%d" % layer, "v%d" % layer]

    def issue_casts(layer):
        if layer >= c.DEPTH:
            return
        for name in layer_weights(layer):
            r = win[name].shape[0]
            for r0 in range(0, r, CH):
                r1 = min(r, r0 + CH)
                g.dma_start(out=wg[name][r0:r1, :], in_=win[name][r0:r1, :]).then_inc(S.semobj["ag_" + name], 16)
            S.ext_event("W:" + name, ("ag_" + name, S.wcast[name]))
            cast_done.add(name)

    cast_done = set()
    issue_casts(0)

    def wflat(name):
        return wg[name].ap().rearrange("r w -> (r w)")

    def wtile(name, off, x):
        return wflat(name)[off:off + 128 * x].rearrange("(p x) -> p x", p=128)

    scratch_sb = sb("bar_sb", [2, 64])

    def barrier():
        S.barrier(scratch_sb[0:1, :], BAR[1:2, :])
        for name in cast_done:
            S.ext_event("W:" + name, ("ag_" + name, S.wcast[name]))

    rr = {"fm": 0, "tm": 0, "t16": 0, "t32": 0}
    dst_keys = {}

    def load_xT(dst, blk, src=None, kc=None):
        src = XT if src is None else src
        kc = KC if kc is None else kc
        t0 = blk * TB
        if "L" in _os.environ.get("KB", ""):
            return
        d3 = dst[:].rearrange("p (k t) -> p k t", k=kc)
        for k0 in range(0, kc, 16):
            k1 = min(kc, k0 + 16)
            S.dma(d3[:, k0:k1, :], src[:, k0:k1, t0:t0 + TB], writes=[(dst.name, k0)])
        dst_keys[dst.name] = [(dst.name, k0) for k0 in range(0, kc, 16)]

    def fm_gemm(wname, fc_list, actT, kc, wpool, evac, base_off=0):
        nw = len(wpool)

        def load(i):
            fc = fc_list[i]
            wt = wpool[i % nw]
            S.dma(wt[:], wtile(wname, base_off + fc * 128 * kc * 128, kc * 128),
                  reads=["W:" + wname], writes=[wt.name])

        n = len(fc_list)
        la = nw - 1
        for i in range(min(la, n)):
            load(i)
        for i in range(n):
            if i + la < n:
                load(i + la)
            wt = wpool[i % nw]
            j = rr["fm"] % 4
            rr["fm"] += 1
            ps = psA[j][:, 0:TB]
            psn = ("psA", j)
            for k in range(kc):
                S.op("pe", lambda e, k=k: e.matmul(ps, lhsT=wt[:, k * 128:(k + 1) * 128],
                                                    rhs=actT[:, k * TB:(k + 1) * TB],
                                                    start=(k == 0), stop=(k == kc - 1)),
                     reads=[wt.name] + dst_keys.get(actT.name, [actT.name]), writes=[psn])
            evac(i, fc_list[i], ps, psn)

    def tm_gemm(wname, K, F, FW, actT, wpool, evac, base_off=0):
        kc = K // 128
        kgs = kgs_of(K)
        KG = kc // kgs
        FQ = F // FW
        nw = len(wpool)
        items = [(fq, kg) for fq in range(FQ) for kg in range(KG)]

        def load(i):
            fq, kg = items[i]
            wt = wpool[i % nw]
            S.dma(wt[:, 0:kgs * FW], wtile(wname, base_off + (fq * KG + kg) * 128 * kgs * FW, kgs * FW),
                  reads=["W:" + wname], writes=[wt.name])

        n = len(items)
        la = nw - 1
        for i in range(min(la, n)):
            load(i)
        for i in range(n):
            if i + la < n:
                load(i + la)
            fq, kg = items[i]
            wt = wpool[i % nw]
            if kg == 0:
                pp = rr["tm"] % 2
                rr["tm"] += 1
            for tt in range(NTT):
                ps = psA[pp * 2 + tt][:, 0:FW]
                psn = ("psA", pp * 2 + tt)
                for k in range(kgs):
                    kk = kg * kgs + k
                    S.op("pe", lambda e, k=k, kk=kk, tt=tt, ps=ps: e.matmul(
                        ps, lhsT=actT[:, kk * TB + tt * 128: kk * TB + tt * 128 + 128],
                        rhs=wt[:, k * FW:(k + 1) * FW], start=(kk == 0), stop=(kk == kc - 1)),
                        reads=[wt.name] + dst_keys.get(actT.name, [actT.name]), writes=[psn])
                if kg == KG - 1:
                    evac(fq, tt, ps, psn)

    def tr_group(srcs, src_res, dst_ap, dst_res, bank, bankname, kind, eng="dve"):
        n = len(srcs)
        for i, sa in enumerate(srcs):
            o = bank[:, i * 128:(i + 1) * 128]
            if kind == "mm":
                S.op("pe", lambda e, sa=sa, o=o: e.matmul(o, lhsT=sa, rhs=identb[:], start=True, stop=True),
                     reads=[src_res, "identb"], writes=[bankname])
            elif kind == "tf":
                S.op("pe", lambda e, sa=sa, o=o: e.transpose(out=o, in_=sa, identity=ident),
                     reads=[src_res, "cst"], writes=[bankname])
            else:
                S.op("pe", lambda e, sa=sa, o=o: e.transpose(out=o, in_=sa, identity=identb[:]),
                     reads=[src_res, "identb"], writes=[bankname])
        if eng == "act":
            S.op("act", lambda e: e.activation(out=dst_ap, in_=bank[:, 0:n * 128], func=AF.Copy),
                 reads=[bankname], writes=[dst_res])
        else:
            S.op(eng, lambda e: e.tensor_copy(out=dst_ap, in_=bank[:, 0:n * 128]), reads=[bankname], writes=[dst_res])

    def store_xT_tile(xn_tile, tok0, st):
        xb = st["xb"]
        xo = st["xo"]
        S.op("act", lambda e: e.activation(out=xb[:], in_=xn_tile[:], func=AF.Copy),
             reads=[xn_tile.name], writes=[xb.name])
        for g0 in range(0, KC, 4):
            n = min(4, KC - g0)
            j = rr["t16"] % 4
            rr["t16"] += 1
            tr_group([xb[:, (g0 + i) * 128:(g0 + i + 1) * 128] for i in range(n)], xb.name,
                     xo[:, g0 * 128:(g0 + n) * 128], (xo.name, g0), psA[j], ("psA", j), "mm",
                     eng=("dve" if (g0 // 4) % 2 == 0 else "act"))
        xo3 = xo[:].rearrange("p (k t) -> p k t", k=KC)
        for k0 in range(0, KC, 16):
            k1 = min(KC, k0 + 16)
            S.dma(XT[:, k0:k1, tok0:tok0 + 128], xo3[:, k0:k1, :],
                  reads=[(xo.name, g0) for g0 in range(k0, k1, 4)], writes=[])

    def stage_init():
        with ExitStack() as st_:
            xt = [sb("i_x%d" % i, [128, D], F32, st_) for i in range(2)]
            st = {"xb": sb("i_xb", [128, D], BF16, st_), "xo": sb("i_xo", [128, D], BF16, st_)}
            for t in range(TOK // 128):
                x_ = xt[t % 2]
                S.dma(x_[:], x_in[t * 128:(t + 1) * 128, :], writes=[x_.name])
                if "X" not in _os.environ.get("KB", ""):
                    S.dma(XR[t * 128:(t + 1) * 128, :], x_[:], reads=[x_.name], writes=[])
                if "T" not in _os.environ.get("KB", ""):
                    store_xT_tile(x_, t * 128, st)
            barrier()

    def stage_ln_(blk, layer, which, final):
        with ExitStack() as st_:
            xr = [sb("l_xr%d" % i, [128, D], F32, st_) for i in range(2)]
            dl = [sb("l_dl%d" % i, [128, D], F32, st_) for i in range(2)]
            grep = sb("l_g", [128, D], F32, st_)
            brep = sb("l_b", [128, D], F32, st_)
            stats = sb("l_st", [128, (D // 512) * 6], F32, st_)
            mv = sb("l_mv", [128, 2], F32, st_)
            rstd = sb("l_rs", [128, 1], F32, st_)
            st = {"xb": sb("l_xb", [128, D], BF16, st_), "xo": sb("l_xo", [128, D], BF16, st_)}
            S.dma(grep[:], ln_in[layer, 2 * which:2 * which + 1, :].to_broadcast([128, D]), writes=[grep.name])
            S.dma(brep[:], ln_in[layer, 2 * which + 1:2 * which + 2, :].to_broadcast([128, D]), writes=[brep.name])
            for tt in range(NTT):
                t = blk * NTT + tt
                x_, d_ = xr[tt % 2], dl[tt % 2]
                S.dma(x_[:], XR[t * 128:(t + 1) * 128, :], reads=[], writes=[x_.name])
                S.dma(d_[:], DELTA[t * 128:(t + 1) * 128, :], reads=[], writes=[d_.name])
                S.op("dve", lambda e: e.scalar_tensor_tensor(out=d_[:], in0=x_[:], scalar=float(c.alpha),
                                                             in1=d_[:], op0=ALU.mult, op1=ALU.add),
                     reads=[x_.name, d_.name], writes=[d_.name])
                for q in range(D // 512):
                    S.op("dve", lambda e, q=q: e.bn_stats(out=stats[:, q * 6:(q + 1) * 6],
                                                         in_=d_[:, q * 512:(q + 1) * 512]),
                         reads=[d_.name], writes=[(stats.name, q)])
                S.op("dve", lambda e: e.bn_aggr(out=mv[:], in_=stats[:]),
                     reads=[(stats.name, q) for q in range(D // 512)], writes=[mv.name])
                S.op("act", lambda e: e.activation(out=rstd[:], in_=mv[:, 1:2], func=AF.Ln, bias=eps_col[:], scale=1.0),
                     reads=[mv.name, "eps_col"], writes=[rstd.name])
                S.op("act", lambda e: e.activation(out=rstd[:], in_=rstd[:], func=AF.Exp, scale=-0.5),
                     reads=[rstd.name], writes=[rstd.name])
                S.op("dve", lambda e: e.tensor_scalar(out=d_[:], in0=d_[:], scalar1=mv[:, 0:1], scalar2=rstd[:],
                                                      op0=ALU.subtract, op1=ALU.mult),
                     reads=[d_.name, mv.name, rstd.name], writes=[d_.name])
                S.op("pool", lambda e: e.tensor_tensor(out=d_[:], in0=d_[:], in1=grep[:], op=ALU.mult),
                     reads=[d_.name, grep.name], writes=[d_.name])
                S.op("pool", lambda e: e.tensor_tensor(out=x_[:], in0=d_[:], in1=brep[:], op=ALU.add),
                     reads=[d_.name, brep.name], writes=[x_.name])
                if final:
                    S.dma(y_out[t * 128:(t + 1) * 128, :], x_[:], reads=[x_.name], writes=[])
                else:
                    S.dma(XR[t * 128:(t + 1) * 128, :], x_[:], reads=[x_.name], writes=[])
                    store_xT_tile(x_, t * 128, st)
            barrier()

    def stage_sc_(blk, j, carry, cw):
        with ExitStack() as st_:
            xT = sb("s_xT", [128, KC * TB], BF16, st_)
            yT = sb("s_yT", [128, KC * TB], BF16, st_)
            wpool = [sb("s_w%d" % i, [128, KC * 128], BF16, st_) for i in range(4)]
            wpool2 = [sb("s_v%d" % i, [128, kgs_of(D) * 512], BF16, st_) for i in range(3)]
            cs = [sb("s_cs%d" % i, [128, TB], F32, st_) for i in range(2)]
            ub = [sb("s_ub%d" % i, [128, TB + 2], F32, st_) for i in range(2)]
            acc = [sb("s_ac%d" % i, [128, TB], F32, st_) for i in range(2)]
            stg = [sb("s_sg%d" % i, [128, 512], F32, st_) for i in range(4)]
            load_xT(xT, blk)
            wn = "sc_wi%d" % j
            order = []
            for fc in range(KC):
                order += [KC + fc, 2 * KC + fc, fc]

            def evac(i, fcw, ps, psn):
                fc = fcw % KC
                kind = fcw // KC
                p = fc % 2
                if kind == 1:
                    S.op("act", lambda e: e.activation(out=cs[p][:], in_=ps, func=AF.Copy),
                         reads=[psn], writes=[cs[p].name])
                elif kind == 2:
                    S.op("dve", lambda e: e.tensor_copy(out=ub[p][:, 0:2], in_=carry[:, 2 * fc:2 * fc + 2]),
                         reads=[("carry", fc)], writes=[(ub[p].name, "h")])
                    S.op("dve", lambda e: e.tensor_tensor(out=ub[p][:, 2:2 + TB], in0=cs[p][:], in1=ps, op=ALU.mult),
                         reads=[cs[p].name, psn], writes=[(ub[p].name, "b")])
                    S.op("dve", lambda e: e.tensor_copy(out=carry[:, 2 * fc:2 * fc + 2], in_=ub[p][:, TB:TB + 2]),
                         reads=[(ub[p].name, "b")], writes=[("carry", fc)])
                    rd = [(ub[p].name, "h"), (ub[p].name, "b"), "sc_cw"]
                    S.op("dve", lambda e: e.tensor_scalar(out=acc[p][:], in0=ub[p][:, 0:TB],
                                                           scalar1=cw[:, 3 * fc:3 * fc + 1], scalar2=None,
                                                           op0=ALU.mult),
                         reads=rd, writes=[acc[p].name])
                    for kk in (1, 2):
                        S.op("dve", lambda e, kk=kk: e.scalar_tensor_tensor(
                            out=acc[p][:], in0=ub[p][:, kk:kk + TB], scalar=cw[:, 3 * fc + kk:3 * fc + kk + 1],
                            in1=acc[p][:], op0=ALU.mult, op1=ALU.add),
                            reads=rd + [acc[p].name], writes=[acc[p].name])
                else:
                    S.op("dve", lambda e: e.tensor_tensor(out=yT[:, fc * TB:(fc + 1) * TB], in0=acc[p][:], in1=ps,
                                                          op=ALU.mult),
                         reads=[acc[p].name, psn], writes=[yT.name])

            fm_gemm(wn, order, xT, KC, wpool, evac)

            def evac2(fq, tt, ps, psn):
                sgi = stg[(fq * NTT + tt) % 4]
                S.op("act", lambda e: e.activation(out=sgi[:], in_=ps, func=AF.Copy), reads=[psn], writes=[sgi.name])
                t = blk * NTT + tt
                S.dma(DELTA[t * 128:(t + 1) * 128, fq * 512:(fq + 1) * 512], sgi[:], reads=[sgi.name],
                      writes=[])

            tm_gemm("sc_wo%d" % j, D, D, 512, yT, wpool2, evac2)
            barrier()

    NXC = CONVD // 128
    NXS = DI // 128

    def stage_ssd1_(blk, j, carry, cw, hp):
        with ExitStack() as st_:
            xT = sb("a_xT", [128, KC * TB], BF16, st_)
            wpool = [sb("a_w%d" % i, [128, KC * 128], BF16, st_) for i in range(4)]
            wpool2 = [sb("a_v%d" % i, [128, kgs_of(D) * 512], BF16, st_) for i in range(3)]
            cb = [sb("a_cb%d" % i, [128, TB + 3], F32, st_) for i in range(2)]
            acc = [sb("a_ac%d" % i, [128, TB], F32, st_) for i in range(2)]
            res = [sb("a_rs%d" % i, [128, TB], F32, st_) for i in range(2)]
            resb = [sb("a_rb%d" % i, [128, TB], BF16, st_) for i in range(2)]
            stg = [sb("a_sg%d" % i, [128, 512], F32, st_) for i in range(4)]
            stgb = [sb("a_sb%d" % i, [128, NTT * 128], BF16, st_) for i in range(2)]
            dtt = [sb("a_dt%d" % i, [128, NH], F32, st_) for i in range(2)]
            dta = [sb("a_da%d" % i, [128, NH], F32, st_) for i in range(2)]
            load_xT(xT, blk)
            t0 = blk * TB

            def evac(i, fc, ps, psn):
                p = i % 2
                S.op("dve", lambda e: e.tensor_copy(out=cb[p][:, 0:3], in_=carry[:, 3 * fc:3 * fc + 3]),
                     reads=[("carry", fc)], writes=[(cb[p].name, "h")])
                S.op("act", lambda e: e.activation(out=cb[p][:, 3:3 + TB], in_=ps, func=AF.Copy),
                     reads=[psn], writes=[(cb[p].name, "b")])
                S.op("dve", lambda e: e.tensor_copy(out=carry[:, 3 * fc:3 * fc + 3], in_=cb[p][:, TB:TB + 3]),
                     reads=[(cb[p].name, "b")], writes=[("carry", fc)])
                rd = [(cb[p].name, "h"), (cb[p].name, "b"), "ssd_cw"]
                S.op("dve", lambda e: e.tensor_scalar(out=acc[p][:], in0=cb[p][:, 0:TB],
                                                       scalar1=cw[:, 5 * fc:5 * fc + 1], scalar2=None, op0=ALU.mult),
                     reads=rd, writes=[acc[p].name])
                for kk in (1, 2, 3):
                    S.op("dve", lambda e, kk=kk: e.scalar_tensor_tensor(
                        out=acc[p][:], in0=cb[p][:, kk:kk + TB], scalar=cw[:, 5 * fc + kk:5 * fc + kk + 1],
                        in1=acc[p][:], op0=ALU.mult, op1=ALU.add),
                        reads=rd + [acc[p].name], writes=[acc[p].name])
                S.op("act", lambda e: e.activation(out=res[p][:], in_=acc[p][:], func=AF.Silu,
                                                   bias=cw[:, 5 * fc + 4:5 * fc + 5], scale=1.0),
                     reads=[acc[p].name, "ssd_cw"], writes=[res[p].name])
                bk = 4 + (i % 2)
                if fc < NXS:
                    sg = stg[i % 4]
                    tr_group([res[p][:, tt * 128:(tt + 1) * 128] for tt in range(NTT)], res[p].name,
                             sg[:, 0:NTT * 128], sg.name, psA[bk], ("psA", bk), "tf")
                    for tt in range(NTT):
                        S.dma(XS[t0 + tt * 128:t0 + (tt + 1) * 128, fc * 128:(fc + 1) * 128],
                              sg[:, tt * 128:(tt + 1) * 128], reads=[sg.name], writes=[])
                else:
                    gi = (fc - NXS) % G
                    isB = (fc - NXS) < G
                    S.op("pool", lambda e: e.tensor_copy(out=resb[p][:], in_=res[p][:]),
                         reads=[res[p].name], writes=[resb[p].name])
                    S.dma((BTd if isB else CTd)[gi, :, t0:t0 + TB], resb[p][:], reads=[resb[p].name],
                          writes=[])
                    if isB:
                        sg = stgb[i % 2]
                        tr_group([resb[p][:, tt * 128:(tt + 1) * 128] for tt in range(NTT)], resb[p].name,
                                 sg[:, 0:NTT * 128], sg.name, psA[bk], ("psA", bk), "mm")
                        for tt in range(NTT):
                            S.dma(BTM[t0 + tt * 128:t0 + (tt + 1) * 128, gi * 128:(gi + 1) * 128],
                                  sg[:, tt * 128:(tt + 1) * 128], reads=[sg.name], writes=[])

            if "a" not in _os.environ.get("KB", ""):
                fm_gemm("ssd_wx%d" % j, list(range(NXC)), xT, KC, wpool, evac)

            def evac_z(fq, tt, ps, psn):
                sg = stg[(fq * NTT + tt) % 4]
                S.op("act", lambda e: e.activation(out=sg[:], in_=ps, func=AF.Silu), reads=[psn], writes=[sg.name])
                S.dma(ZS[t0 + tt * 128:t0 + (tt + 1) * 128, fq * 512:(fq + 1) * 512], sg[:], reads=[sg.name],
                      writes=[])

            if "z" not in _os.environ.get("KB", ""):
                tm_gemm("ssd_wz%d" % j, D, DI, 512, xT, wpool2, evac_z)

            def evac_dt(fq, tt, ps, psn):
                d_, a_ = dtt[tt % 2], dta[tt % 2]
                S.op("dve", lambda e: e.tensor_tensor(out=d_[:], in0=ps, in1=hp[:, 0:NH], op=ALU.add),
                     reads=[psn, "ssd_hp"], writes=[d_.name])
                S.op("act", lambda e: e.activation(out=d_[:], in_=d_[:], func=AF.Exp), reads=[d_.name], writes=[d_.name])
                S.op("act", lambda e: e.activation(out=d_[:], in_=d_[:], func=AF.Ln, bias=one_col[:], scale=1.0),
                     reads=[d_.name, "one_col"], writes=[d_.name])
                S.op("dve", lambda e: e.tensor_tensor(out=a_[:], in0=d_[:], in1=hp[:, NH:2 * NH], op=ALU.mult),
                     reads=[d_.name, "ssd_hp"], writes=[a_.name])
                S.dma(DTd[t0 + tt * 128:t0 + (tt + 1) * 128, :], d_[:], reads=[d_.name], writes=[])
                S.dma(DAd[t0 + tt * 128:t0 + (tt + 1) * 128, :], a_[:], reads=[a_.name], writes=[])

            if "d" not in _os.environ.get("KB", ""):
                tm_gemm("ssd_wdt%d" % j, D, NH, NH, xT, wpool2, evac_dt)
            barrier()

    HQ = min(4, HPG)

    def stage_ssd2_(ch, j, ST32, STb, hp, nwrep, bufs):
        t0 = ch * 128
        b = bufs
        dt_, da_ = b["dt"], b["da"]
        S.dma(dt_[:], DTd[t0:t0 + 128, :], reads=[], writes=[dt_.name])
        S.dma(da_[:], DAd[t0:t0 + 128, :], reads=[], writes=[da_.name])
        eacs, eend, etot = b["eacs"], b["eend"], b["etot"]
        for (lh, dst, jx) in ((tri, eacs, 0), (Umat, eend, 1), (ones, etot, 2)):
            ps = psF[:, jx * 128:jx * 128 + NH]
            S.op("pe", lambda e, lh=lh, ps=ps: e.matmul(ps, lhsT=lh, rhs=da_[:], start=True, stop=True),
                 reads=["cst", da_.name], writes=[("psF", 0)])
        for (lh, dst, jx) in ((tri, eacs, 0), (Umat, eend, 1), (ones, etot, 2)):
            ps = psF[:, jx * 128:jx * 128 + NH]
            S.op("act", lambda e, ps=ps, dst=dst: e.activation(out=dst[:], in_=ps, func=AF.Exp),
                 reads=[("psF", 0)], writes=[dst.name])
        for gi in range(G):
            p = gi % 2
            xs, zs, bt, ct, btm = b["xs"][p], b["zs"][p], b["bt"][p], b["ct"][p], b["btm"][p]
            S.dma(xs[:], XS[t0:t0 + 128, gi * GW:(gi + 1) * GW], reads=[], writes=[xs.name])
            S.dma(zs[:], ZS[t0:t0 + 128, gi * GW:(gi + 1) * GW], reads=[], writes=[zs.name])
            S.dma(bt[:], BTd[gi, :, t0:t0 + 128], reads=[], writes=[bt.name])
            S.dma(ct[:], CTd[gi, :, t0:t0 + 128], reads=[], writes=[ct.name])
            S.dma(btm[:], BTM[t0:t0 + 128, gi * 128:(gi + 1) * 128], reads=[], writes=[btm.name])
            h0 = gi * HPG
            xdt, xdtw = b["xdt"][p], b["xdtw"][p]
            v3 = lambda t_: t_[:].rearrange("p (j d) -> p j d", j=HPG)
            bc = lambda col: col[:, h0:h0 + HPG].unsqueeze(2).to_broadcast([128, HPG, c.HD])
            S.op("dve", lambda e: e.tensor_tensor(out=v3(xdt), in0=v3(xs), in1=bc(dt_), op=ALU.mult),
                 reads=[xs.name, dt_.name], writes=[xdt.name])
            S.op("pool", lambda e: e.tensor_tensor(out=v3(xdtw), in0=v3(xdt), in1=bc(eend), op=ALU.mult),
                 reads=[xdt.name, eend.name], writes=[xdtw.name])
            cbm = b["cbm"][p]
            S.op("pe", lambda e: e.matmul(psA[4][:, 0:128], lhsT=bt[:], rhs=ct[:], start=True, stop=True),
                 reads=[bt.name, ct.name], writes=[("psA", 4)])
            S.op("dve", lambda e: e.tensor_tensor(out=cbm[:], in0=psA[4][:, 0:128], in1=tri, op=ALU.mult),
                 reads=[("psA", 4), "cst"], writes=[cbm.name])
            R = b["R"][p]
            S.op("pool", lambda e: e.tensor_tensor(
                out=R[:].rearrange("p (j l) -> p j l", j=HPG),
                in0=da_[:, h0:h0 + HPG].unsqueeze(2).to_broadcast([128, HPG, 128]),
                in1=tri.unsqueeze(1).to_broadcast([128, HPG, 128]), op=ALU.mult),
                reads=[da_.name, "cst"], writes=[R.name])
            nyb = (GW + 511) // 512
            for hq in range(HPG // HQ):
                dec, wT = b["dec"][hq % 2], b["wT"][hq % 2]
                W_ = HQ * 128
                S.op("pe", lambda e, hq=hq: e.matmul(psA[5][:, 0:W_], lhsT=Umat, rhs=R[:, hq * W_:(hq + 1) * W_],
                                                      start=True, stop=True),
                     reads=["cst", R.name], writes=[("psA", 5)])
                S.op("act", lambda e: e.activation(out=dec[:, 0:W_], in_=psA[5][:, 0:W_], func=AF.Exp),
                     reads=[("psA", 5)], writes=[dec.name])
                S.op("dve", lambda e: e.tensor_tensor(
                    out=wT[:, 0:W_].rearrange("p (j l) -> p j l", j=HQ),
                    in0=dec[:, 0:W_].rearrange("p (j l) -> p j l", j=HQ),
                    in1=cbm[:].unsqueeze(1).to_broadcast([128, HQ, 128]), op=ALU.mult),
                    reads=[dec.name, cbm.name], writes=[wT.name])
                for jj in range(HQ):
                    hh = hq * HQ + jj
                    col = hh * c.HD
                    S.op("pe", lambda e, jj=jj, col=col: e.matmul(
                        psA[col // 512][:, col % 512:col % 512 + c.HD], lhsT=wT[:, jj * 128:(jj + 1) * 128],
                        rhs=xdt[:, col:col + c.HD], start=True, stop=True),
                        reads=[wT.name, xdt.name], writes=[("psA", col // 512)])
            for q in range(nyb):
                wq_ = min(512, GW - q * 512)
                S.op("pe", lambda e, q=q, wq_=wq_: e.matmul(psA[2 + q][:, 0:wq_], lhsT=ct[:],
                                                           rhs=STb[:, gi * GW + q * 512: gi * GW + q * 512 + wq_],
                                                           start=True, stop=True),
                     reads=[ct.name, ("STb", gi)], writes=[("psA", 2 + q)])
            y = b["y"][p]
            for q in range(nyb):
                wq_ = min(512, GW - q * 512)
                nh_ = wq_ // c.HD
                hs = h0 + q * (512 // c.HD)
                S.op("dve", lambda e, q=q, wq_=wq_, nh_=nh_, hs=hs: e.tensor_tensor(
                    out=y[:, q * 512:q * 512 + wq_].rearrange("p (j d) -> p j d", j=nh_),
                    in0=psA[2 + q][:, 0:wq_].rearrange("p (j d) -> p j d", j=nh_),
                    in1=eacs[:, hs:hs + nh_].unsqueeze(2).to_broadcast([128, nh_, c.HD]), op=ALU.mult),
                    reads=[("psA", 2 + q), eacs.name], writes=[(y.name, q)])
                S.op("dve", lambda e, q=q, wq_=wq_: e.tensor_tensor(
                    out=y[:, q * 512:q * 512 + wq_], in0=y[:, q * 512:q * 512 + wq_], in1=psA[q][:, 0:wq_],
                    op=ALU.add), reads=[(y.name, q), ("psA", q)], writes=[(y.name, q)])
            yr = [(y.name, q) for q in range(nyb)]
            tmp = b["tmp"][p]
            S.op("pool", lambda e: e.tensor_tensor(out=v3(tmp), in0=v3(xs), in1=bc(hp[:, 2 * NH:3 * NH]) if False else
                                                   hp[:, 2 * NH + h0:2 * NH + h0 + HPG].unsqueeze(2).to_broadcast(
                                                       [128, HPG, c.HD]), op=ALU.mult),
                 reads=[xs.name, "ssd_hp"], writes=[tmp.name])
            S.op("pool", lambda e: e.tensor_tensor(out=y[:], in0=y[:], in1=tmp[:], op=ALU.add),
                 reads=yr + [tmp.name], writes=yr)
            S.op("dve", lambda e: e.tensor_tensor(out=y[:], in0=y[:], in1=zs[:], op=ALU.mult),
                 reads=yr + [zs.name], writes=yr)
            ss, rs = b["ss"][p], b["rs"][p]
            S.op("dve", lambda e: e.memset(ss[:], 0.0), writes=[ss.name])
            S.op("act", lambda e: e.activation(out=tmp[:], in_=y[:], func=AF.Square, accum_out=ss[:]),
                 reads=yr + [ss.name], writes=[tmp.name, ss.name])
            S.op("act", lambda e: e.activation(out=rs[:], in_=ss[:], func=AF.Ln, bias=eps_col[:], scale=1.0 / GW),
                 reads=[ss.name, "eps_col"], writes=[rs.name])
            S.op("act", lambda e: e.activation(out=rs[:], in_=rs[:], func=AF.Exp, scale=-0.5),
                 reads=[rs.name], writes=[rs.name])
            hn = b["hn"][p]
            S.op("dve", lambda e: e.scalar_tensor_tensor(out=hn[:], in0=y[:], scalar=rs[:],
                                                         in1=nwrep[:, gi * GW:(gi + 1) * GW],
                                                         op0=ALU.mult, op1=ALU.mult),
                 reads=yr + [rs.name, "nwrep"], writes=[hn.name])
            ho = b["ho"][p]
            tr_group([hn[:, k * 128:(k + 1) * 128] for k in range(GW // 128)], hn.name, ho[:], ho.name,
                     psT, ("psT", 0), "tb", eng=("act" if gi % 2 else "dve"))
            S.dma(HT[:, gi * (GW // 128):(gi + 1) * (GW // 128), t0:t0 + 128],
                  ho[:].rearrange("p (k t) -> p k t", k=GW // 128), reads=[ho.name], writes=[])
            for q in range(nyb):
                wq_ = min(512, GW - q * 512)
                S.op("pe", lambda e, q=q, wq_=wq_: e.matmul(psA[2 + q][:, 0:wq_], lhsT=btm[:],
                                                           rhs=xdtw[:, q * 512:q * 512 + wq_], start=True, stop=True),
                     reads=[btm.name, xdtw.name], writes=[("psA", 2 + q)])
            sg = ST32[:, gi * GW:(gi + 1) * GW]
            S.op("pool", lambda e: e.tensor_tensor(
                out=sg.rearrange("p (j d) -> p j d", j=HPG), in0=sg.rearrange("p (j d) -> p j d", j=HPG),
                in1=etot[:, h0:h0 + HPG].unsqueeze(2).to_broadcast([128, HPG, c.HD]), op=ALU.mult),
                reads=[("ST32", gi), etot.name], writes=[("ST32", gi)])
            for q in range(nyb):
                wq_ = min(512, GW - q * 512)
                S.op("dve", lambda e, q=q, wq_=wq_: e.tensor_tensor(
                    out=sg[:, q * 512:q * 512 + wq_], in0=sg[:, q * 512:q * 512 + wq_], in1=psA[2 + q][:, 0:wq_],
                    op=ALU.add), reads=[("ST32", gi), ("psA", 2 + q)], writes=[("ST32", gi)])
            S.op("act", lambda e: e.activation(out=STb[:, gi * GW:(gi + 1) * GW], in_=sg, func=AF.Copy),
                 reads=[("ST32", gi)], writes=[("STb", gi)])

    def stage_ssd3_(blk, j):
        with ExitStack() as st_:
            kc = DI // 128
            hT = sb("c_hT", [128, kc * TB], BF16, st_)
            wpool2 = [sb("c_v%d" % i, [128, kgs_of(DI) * 512], BF16, st_) for i in range(3)]
            stg = [sb("c_sg%d" % i, [128, 512], F32, st_) for i in range(4)]
            load_xT(hT, blk, src=HT, kc=kc)

            def evac2(fq, tt, ps, psn):
                sgi = stg[(fq * NTT + tt) % 4]
                S.op("act", lambda e: e.activation(out=sgi[:], in_=ps, func=AF.Copy), reads=[psn], writes=[sgi.name])
                t = blk * NTT + tt
                S.dma(DELTA[t * 128:(t + 1) * 128, fq * 512:(fq + 1) * 512], sgi[:], reads=[sgi.name],
                      writes=[])

            tm_gemm("ssd_wo%d" % j, DI, D, 512, hT, wpool2, evac2)
            barrier()

    PH = c.PH

    def stage_peer_(blk, layer, skT):
        with ExitStack() as st_:
            xT = sb("p_xT", [128, KC * TB], BF16, st_)
            sc = [sb("p_sc%d" % i, [128, 16 * 128], F32, st_) for i in range(NTT)]
            A2 = [sb("p_A%d" % i, [128, 2 * PH], F32, st_) for i in range(NTT)]
            dg = [sb("p_dg%d" % i, [128, PH * 128], BF16, st_) for i in range(NTT)]
            load_xT(xT, blk)
            with ExitStack() as sa:
                qT = sb("p_qT", [128, 16 * TB], F32, sa)
                wpool = [sb("p_w%d" % i, [128, KC * 128], BF16, sa) for i in range(3)]
                t16 = sb("p_t16", [128, 16 * 16], F32, sa)
                wk = sb("p_wk", [128, 256], F32, sa)
                cand = sb("p_cand", [128, PH * 256], F32, sa)
                tc16 = sb("p_tc", [128, PH * 16], F32, sa)
                ex = sb("p_ex", [128, PH * 16], F32, sa)
                Z = sb("p_Z", [128, PH], F32, sa)
                cc = sb("p_cc", [128, PH], F32, sa)

                def evq(i, fc, ps, psn):
                    S.op("act", lambda e: e.activation(out=qT[:, fc * TB:(fc + 1) * TB], in_=ps, func=AF.Copy),
                         reads=[psn], writes=[(qT.name, fc)])

                fm_gemm("wq%d" % layer, list(range(16)), xT, KC, wpool, evq)
                for tt in range(NTT):
                    for q4 in range(4):
                        for u_ in range(4):
                            qc = q4 * 4 + u_
                            S.op("pe", lambda e, qc=qc, u_=u_, tt=tt: e.matmul(
                                psA[tt][:, u_ * 128:(u_ + 1) * 128],
                                lhsT=qT[:, qc * TB + tt * 128: qc * TB + tt * 128 + 128],
                                rhs=skT[:, qc * 128:(qc + 1) * 128], start=True, stop=True),
                                reads=[(qT.name, qc), "skT"], writes=[("psA", tt)])
                        S.op("act", lambda e, q4=q4, tt=tt: e.activation(out=sc[tt][:, q4 * 512:(q4 + 1) * 512],
                                                                         in_=psA[tt][:, :], func=AF.Copy),
                             reads=[("psA", tt)], writes=[sc[tt].name])
                    s_ = sc[tt]
                    for hi in range(16):
                        src = s_[:, hi * 128:(hi + 1) * 128]
                        S.op("dve", lambda e, hi=hi, src=src: e.max(out=t16[:, hi * 16:hi * 16 + 8], in_=src),
                             reads=[s_.name], writes=[(t16.name, hi)])
                        S.op("dve", lambda e, hi=hi, src=src: e.match_replace(
                            out=wk[:, 0:128], in_to_replace=t16[:, hi * 16:hi * 16 + 8], in_values=src,
                            imm_value=-1e30), reads=[s_.name, (t16.name, hi)], writes=[wk.name])
                        S.op("dve", lambda e, hi=hi: e.max(out=t16[:, hi * 16 + 8:hi * 16 + 16], in_=wk[:, 0:128]),
                             reads=[wk.name], writes=[(t16.name, hi)])
                    tr = [(t16.name, hi) for hi in range(16)]
                    tv = t16[:].rearrange("p (h i j) -> p h i j", h=PH, i=2)
                    S.op("dve", lambda e: e.tensor_tensor(
                        out=cand[:].rearrange("p (h a b) -> p h a b", h=PH, a=16),
                        in0=tv[:, :, 0, :].unsqueeze(3).to_broadcast([128, PH, 16, 16]),
                        in1=tv[:, :, 1, :].unsqueeze(2).to_broadcast([128, PH, 16, 16]), op=ALU.add),
                        reads=tr, writes=[cand.name])
                    for h in range(PH):
                        src = cand[:, h * 256:(h + 1) * 256]
                        S.op("dve", lambda e, h=h, src=src: e.max(out=tc16[:, h * 16:h * 16 + 8], in_=src),
                             reads=[cand.name], writes=[(tc16.name, h)])
                        S.op("dve", lambda e, h=h, src=src: e.match_replace(
                            out=wk[:, 0:256], in_to_replace=tc16[:, h * 16:h * 16 + 8], in_values=src,
                            imm_value=-1e30), reads=[cand.name, (tc16.name, h)], writes=[wk.name])
                        S.op("dve", lambda e, h=h: e.max(out=tc16[:, h * 16 + 8:h * 16 + 16], in_=wk[:, 0:256]),
                             reads=[wk.name], writes=[(tc16.name, h)])
                    tcr = [(tc16.name, h) for h in range(PH)]
                    tcv = tc16[:].rearrange("p (h k) -> p h k", h=PH)
                    S.op("dve", lambda e: e.tensor_tensor(
                        out=ex[:].rearrange("p (h k) -> p h k", h=PH), in0=tcv,
                        in1=tcv[:, :, 0:1].to_broadcast([128, PH, 16]), op=ALU.subtract),
                        reads=tcr, writes=[ex.name])
                    S.op("act", lambda e: e.activation(out=ex[:], in_=ex[:], func=AF.Exp), reads=[ex.name],
                         writes=[ex.name])
                    S.op("dve", lambda e: e.tensor_reduce(out=Z[:], in_=ex[:].rearrange("p (h k) -> p h k", h=PH),
                                                          axis=AX.X, op=ALU.add), reads=[ex.name], writes=[Z.name])
                    S.op("dve", lambda e: e.reciprocal(out=Z[:], in_=Z[:]), reads=[Z.name], writes=[Z.name])
                    S.op("dve", lambda e: e.tensor_tensor(
                        out=cc[:], in0=ex[:].rearrange("p (h k) -> p h k", h=PH)[:, :, 15], in1=Z[:], op=ALU.mult),
                        reads=[ex.name, Z.name], writes=[cc.name])
                    S.op("dve", lambda e, tt=tt: e.tensor_copy(out=A2[tt][:, 0:PH], in_=tcv[:, :, 15]),
                         reads=tcr, writes=[A2[tt].name])
                    S.op("dve", lambda e, tt=tt: e.tensor_scalar(out=A2[tt][:, PH:2 * PH], in0=tcv[:, :, 15],
                                                                 scalar1=-1.0, scalar2=None, op0=ALU.mult),
                         reads=tcr, writes=[A2[tt].name])
                    for h in range(PH):
                        S.op("dve", lambda e, h=h, tt=tt: e.tensor_scalar(
                            out=dg[tt][:, h * 128:(h + 1) * 128], in0=ident, scalar1=cc[:, h:h + 1], scalar2=None,
                            op0=ALU.mult), reads=["cst", cc.name], writes=[(dg[tt].name, h)])
                barrier()
            with ExitStack() as sh:
                HdT = sb("p_Hd", [128, 128 * TB], BF16, sh)
                with ExitStack() as sbk:
                    upool = [sb("p_u%d" % i, [128, KC * 128], BF16, sbk) for i in range(3)]
                    Sb = [sb("p_S%d" % i, [128, PH * 2 * 128], F32, sbk) for i in range(2)]
                    Eb = [sb("p_E%d" % i, [128, PH * 2 * 128], BF16, sbk) for i in range(2)]
                    Mb = [sb("p_M%d" % i, [128, PH * 2 * 128], BF16, sbk) for i in range(2)]
                    gel = [sb("p_g%d" % i, [128, TB], F32, sbk) for i in range(2)]
                    pre_ps = {}

                    def evu(i, k1, ps, psn):
                        gi_ = 4 + (k1 // 2) % 2
                        if k1 % 2 == 0:
                            for tt in range(NTT):
                                ib = ((k1 // 2) * NTT + tt) % 2
                                S_, E_, M_ = Sb[ib], Eb[ib], Mb[ib]
                                svw = sc[tt][:].rearrange("p (h i k) -> p h i k", h=PH, i=2)
                                S.op("pool", lambda e, tt=tt, S_=S_, svw=svw: e.tensor_tensor(
                                    out=S_[:].rearrange("p (h a k) -> p h a k", h=PH, a=2),
                                    in0=svw[:, :, 0, k1:k1 + 2].unsqueeze(3).to_broadcast([128, PH, 2, 128]),
                                    in1=svw[:, :, 1, :].unsqueeze(2).to_broadcast([128, PH, 2, 128]), op=ALU.add),
                                    reads=[sc[tt].name], writes=[S_.name])
                                for h in range(PH):
                                    S.op("act", lambda e, S_=S_, E_=E_, h=h, tt=tt: e.activation(
                                        out=E_[:, h * 256:(h + 1) * 256], in_=S_[:, h * 256:(h + 1) * 256],
                                        func=AF.Exp, bias=A2[tt][:, PH + h:PH + h + 1], scale=1.0),
                                        reads=[S_.name, A2[tt].name], writes=[(E_.name, h)])
                                    S.op("dve", lambda e, S_=S_, E_=E_, M_=M_, h=h, tt=tt: e.scalar_tensor_tensor(
                                        out=M_[:, h * 256:(h + 1) * 256], in0=S_[:, h * 256:(h + 1) * 256],
                                        scalar=A2[tt][:, h:h + 1], in1=E_[:, h * 256:(h + 1) * 256],
                                        op0=ALU.is_ge, op1=ALU.mult),
                                        reads=[S_.name, (E_.name, h), A2[tt].name], writes=[(M_.name, h)])
                                for a in range(2):
                                    for h in range(PH):
                                        o0 = (a * NTT + tt) * 128
                                        S.op("pe", lambda e, a=a, h=h, tt=tt, M_=M_, o0=o0: e.matmul(
                                            psA[gi_][:, o0:o0 + 128],
                                            lhsT=M_[:, (h * 2 + a) * 128:(h * 2 + a + 1) * 128],
                                            rhs=dg[tt][:, h * 128:(h + 1) * 128], start=(h == 0), stop=(h == PH - 1)),
                                            reads=[(M_.name, h), (dg[tt].name, h)],
                                            writes=[("psA", gi_)])
                        gl = gel[k1 % 2]
                        S.op("act", lambda e: e.activation(out=gl[:], in_=ps, func=AF.Gelu), reads=[psn],
                             writes=[gl.name])
                        a = k1 % 2
                        S.op("dve", lambda e: e.tensor_tensor(
                            out=HdT[:, k1 * TB:(k1 + 1) * TB], in0=gl[:],
                            in1=psA[gi_][:, a * TB:(a + 1) * TB], op=ALU.mult),
                            reads=[gl.name, ("psA", gi_)], writes=[HdT.name])

                    fm_gemm("ut%d" % layer, list(range(128)), xT, KC, upool, evu)
                    barrier()
                with ExitStack() as sc_:
                    vpool = [sb("p_v%d" % i, [128, 16 * 512], BF16, sc_) for i in range(3)]
                    stg = [sb("p_sg%d" % i, [128, 512], F32, sc_) for i in range(4)]

                    def evv(fq, tt, ps, psn):
                        sgi = stg[(fq * NTT + tt) % 4]
                        S.op("act", lambda e: e.activation(out=sgi[:], in_=ps, func=AF.Copy), reads=[psn],
                             writes=[sgi.name])
                        t = blk * NTT + tt
                        S.dma(DELTA[t * 128:(t + 1) * 128, fq * 512:(fq + 1) * 512], sgi[:], reads=[sgi.name],
                              writes=[])

                    tm_gemm("v%d" % layer, c.E, D, 512, HdT, vpool, evv)
                    barrier()

    def _forward():
        for layer in range(c.DEPTH):
            j = layer // 2
            issue_casts(layer + 1)
            if layer % 2 == 0:
                with ExitStack() as sl:
                    carry = sb("q_carry", [128, NXC * 3], F32, sl)
                    cw = sb("q_cw", [128, NXC * 5], F32, sl)
                    hp = sb("q_hp", [128, 3 * NH], F32, sl)
                    S.op("dve", lambda e: e.memset(carry[:], 0.0), writes=[("carry", fc) for fc in range(NXC)])
                    S.dma(cw[:], ssd_cw_in[j], writes=["ssd_cw"])
                    S.dma(hp[:], ssd_hp_in[j:j + 1].rearrange("o a n -> o (a n)").to_broadcast([128, 3 * NH]),
                          writes=["ssd_hp"])
                    S.op("act", lambda e: e.activation(out=hp[:, NH:2 * NH], in_=hp[:, NH:2 * NH], func=AF.Exp),
                         reads=["ssd_hp"], writes=["ssd_hp"])
                    S.op("dve", lambda e: e.tensor_scalar(out=hp[:, NH:2 * NH], in0=hp[:, NH:2 * NH], scalar1=-1.0,
                                                          scalar2=None, op0=ALU.mult), reads=["ssd_hp"], writes=["ssd_hp"])
                    barrier()
                    S.res["ssd_cw"] = [None, {}]
                    for blk in range(NB):
                        stage_ssd1(blk, j, carry, cw, hp)
                    with ExitStack() as s2:
                        ST32 = sb("q_ST", [128, G * GW], F32, s2)
                        STb = sb("q_STb", [128, G * GW], BF16, s2)
                        nwrep = sb("q_nw", [128, DI], F32, s2)
                        S.op("dve", lambda e: e.memset(ST32[:], 0.0), writes=[("ST32", gi) for gi in range(G)])
                        S.op("pool", lambda e: e.memset(STb[:], 0.0), writes=[("STb", gi) for gi in range(G)])
                        S.dma(nwrep[:], ssd_nw_in[j:j + 1, :].to_broadcast([128, DI]), writes=["nwrep"])
                        bufs = {"dt": sb("r_dt", [128, NH], F32, s2), "da": sb("r_da", [128, NH], F32, s2),
                                "eacs": sb("r_ea", [128, NH], F32, s2), "eend": sb("r_ee", [128, NH], F32, s2),
                                "etot": sb("r_et", [128, NH], F32, s2)}
                        for nm, shp, dt_ in (("xs", GW, F32), ("zs", GW, F32), ("bt", 128, BF16), ("ct", 128, BF16),
                                             ("btm", 128, BF16), ("xdt", GW, BF16), ("xdtw", GW, BF16),
                                             ("cbm", 128, F32), ("R", HPG * 128, F32), ("dec", HQ * 128, F32),
                                             ("wT", HQ * 128, BF16), ("y", GW, F32), ("tmp", GW, F32), ("ss", 1, F32),
                                             ("rs", 1, F32), ("hn", GW, BF16), ("ho", GW, BF16)):
                            bufs[nm] = [sb("r_%s%d" % (nm, i), [128, shp], dt_, s2) for i in range(2)]
                        for ch in range(TOK // 128):
                            stage_ssd2(ch, j, ST32, STb, hp, nwrep, bufs)
                        barrier()
                for blk in range(NB):
                    stage_ssd3(blk, j)
                    stage_ln(blk, layer, 0, False)
            else:
                with ExitStack() as sl:
                    carry = sb("q_carry2", [128, KC * 2], F32, sl)
                    cw = sb("q_cw2", [128, KC * 3], F32, sl)
                    S.op("dve", lambda e: e.memset(carry[:], 0.0), writes=[("carry", fc) for fc in range(KC)])
                    S.dma(cw[:], sc_cw_in[j], writes=["sc_cw"])
                    barrier()
                    for blk in range(NB):
                        stage_sc(blk, j, carry, cw)
                        stage_ln(blk, layer, 0, False)
            with ExitStack() as sl:
                skT = sb("q_sk", [128, 16 * 128], F32, sl)
                S.dma(skT[:], sk_in[layer], writes=["skT"])
                barrier()
                for blk in range(NB):
                    stage_peer(blk, layer, skT)
                    stage_ln(blk, layer, 1, layer == c.DEPTH - 1)

    import os
    kstop = int(os.environ.get("KSTOP", "-1"))

    class _Stop(Exception):
        pass

    stage_ctr = [0]

    def _wrap(fn):
        def w(*a, **k):
            if kstop >= 0 and stage_ctr[0] >= kstop:
                return None
            stage_ctr[0] += 1
            return fn(*a, **k)
        return w

    stage_ssd1 = _wrap(stage_ssd1_)
    stage_ssd2 = _wrap(stage_ssd2_)
    stage_ssd3 = _wrap(stage_ssd3_)
    stage_ln = _wrap(stage_ln_)
    stage_sc = _wrap(stage_sc_)
    stage_peer = _wrap(stage_peer_)
    stage_init()
    if not os.environ.get('KSKIPFWD'):
        _forward()
    if kstop >= 0:
        barrier()
        with ExitStack() as sx:
            tb_ = sb("dbg_t", [128, D], F32, sx)
            for t in range(TOK // 128):
                S.dma(tb_[:], XR[t * 128:(t + 1) * 128, :], writes=[tb_.name])
                S.dma(y_out[t * 128:(t + 1) * 128, :], tb_[:], reads=[tb_.name], writes=[])
    for name in cast_done:
        S._wait("sp", ("ag_" + name, S.wcast[name]))
    barrier()
    es.close()
    return nc, [n for n, _ in wspecs]


def prepare_inputs(cfg, x, ssd_in_proj, ssd_conv_w, ssd_conv_b, ssd_dt_bias, ssd_A_log, ssd_D, ssd_norm_w,
                   ssd_out_proj, sc_in_proj, sc_conv_w, sc_out_proj, peer_wq, peer_subkeys,
                   peer_u, peer_v, ln1_g, ln1_b, ln2_g, ln2_b):
    c = cfg
    D, DI, NH, CONVD, KC = c.D, c.DI, c.NH, c.CONVD, c.KC
    f = lambda a: np.asarray(a, dtype=np.float32)
    blobs = {}
    n_ssd = (c.DEPTH + 1) // 2
    n_sc = c.DEPTH // 2
    for j in range(n_ssd):
        W = f(ssd_in_proj[j])
        blobs["ssd_wz%d" % j] = shard_blob(tm_tiles(W[:, :DI], 512))
        blobs["ssd_wx%d" % j] = shard_blob(fm_tiles(W[:, DI:DI + CONVD]))
        blobs["ssd_wdt%d" % j] = shard_blob(tm_tiles(W[:, DI + CONVD:], NH))
        blobs["ssd_wo%d" % j] = shard_blob(tm_tiles(f(ssd_out_proj[j]), 512))
    for j in range(n_sc):
        blobs["sc_wi%d" % j] = shard_blob(fm_tiles(f(sc_in_proj[j])))
        blobs["sc_wo%d" % j] = shard_blob(tm_tiles(f(sc_out_proj[j]), 512))
    for i in range(c.DEPTH):
        blobs["wq%d" % i] = shard_blob(fm_tiles(f(peer_wq[i])))
        blobs["ut%d" % i] = shard_blob(fm_tiles(f(peer_u[i]).T))
        blobs["v%d" % i] = shard_blob(tm_tiles(f(peer_v[i]), 512))
    common = {}
    eye = np.eye(128, dtype=np.float32)
    tri = np.triu(np.ones((128, 128), np.float32))
    U = 1.0 - tri
    common["consts"] = np.concatenate([eye, tri, U, np.ones((128, 128), np.float32)], axis=1)
    sk = f(peer_subkeys)
    common["skT"] = np.ascontiguousarray(sk.transpose(0, 4, 1, 2, 3).reshape(c.DEPTH, 128, 16 * 128))
    common["lnp"] = np.ascontiguousarray(np.stack([f(ln1_g), f(ln1_b), f(ln2_g), f(ln2_b)], axis=1))
    if n_ssd:
        cw = f(ssd_conv_w)
        cb = f(ssd_conv_b)
        allc = np.concatenate([cw, cb[:, None, :]], axis=1)
        allc = allc.reshape(n_ssd, 5, CONVD // 128, 128).transpose(0, 3, 2, 1)
        common["ssd_cw"] = np.ascontiguousarray(allc.reshape(n_ssd, 128, -1))
        common["ssd_hp"] = np.ascontiguousarray(np.stack([f(ssd_dt_bias), f(ssd_A_log), f(ssd_D)], axis=1))
        common["ssd_nw"] = f(ssd_norm_w)
    if n_sc:
        w = f(sc_conv_w).reshape(n_sc, 3, KC, 128).transpose(0, 3, 2, 1)
        common["sc_cw"] = np.ascontiguousarray(w.reshape(n_sc, 128, -1))
    xs = f(x)
    in_maps = []
    for core in range(NCORES):
        m = dict(common)
        m["x"] = np.ascontiguousarray(xs[core % c.BATCH])
        for k, v in blobs.items():
            m[k] = v
        in_maps.append(m)
    return in_maps


_CACHE = {}


def run(cfg, inputs):
    in_maps = prepare_inputs(cfg, **inputs)
    key = (cfg.D, cfg.SEQ, cfg.DEPTH, cfg.DI)
    if key not in _CACHE:
        _CACHE[key] = build_program(cfg)
    nc, _ = _CACHE[key]
    res = run_bass_kernel_spmd(nc, in_maps, core_ids=list(range(NCORES)))
    out = np.stack([np.asarray(res.results[b]["y"], dtype=np.float32) for b in range(cfg.BATCH)], axis=0)
    return out


def kernel(**inputs):
    return run(Cfg(), inputs)
```

```python
import math
from contextlib import ExitStack
import numpy as np
import concourse.bass as bass
import concourse.mybir as mybir
from concourse.bass_utils import run_bass_kernel_spmd

F32 = mybir.dt.float32
BF16 = mybir.dt.bfloat16
AF = mybir.ActivationFunctionType
ALU = mybir.AluOpType
AX = mybir.AxisListType

NCORES = 4
RW = 4096
TB = 256
NDS = 40


class Cfg:
    def __init__(s, D=4096, SEQ=4096, BATCH=4, DEPTH=4, DI=8192, NH=128, G=8, NST=128,
                 PH=8, NK=128, TOPK=16):
        s.D, s.SEQ, s.BATCH, s.DEPTH = D, SEQ, BATCH, DEPTH
        s.DI, s.NH, s.G, s.NST = DI, NH, G, NST
        s.HD = DI // NH
        s.HPG = NH // G
        s.GW = s.HPG * s.HD
        s.CONVD = DI + 2 * G * NST
        s.PH, s.NK, s.TOPK = PH, NK, TOPK
        s.E = NK * NK
        s.QD = PH * 2 * 128
        s.TOK = SEQ
        s.KC = D // 128
        s.alpha = (2 * DEPTH) ** 0.25
        assert s.HD == 64 and NST == 128 and NK == 128 and TOPK == 16


def fm_tiles(W):
    K, F = W.shape
    return np.ascontiguousarray(W.reshape(K // 128, 128, F // 128, 128).transpose(2, 1, 0, 3))


def kgs_of(K):
    return min(16, K // 128)


def tm_tiles(W, FW):
    K, F = W.shape
    kgs = kgs_of(K)
    KG = K // 128 // kgs
    return np.ascontiguousarray(W.reshape(KG, kgs, 128, F // FW, FW).transpose(3, 0, 2, 1, 4))


def shard_blob(a):
    flat = a.reshape(-1)
    n = (flat.size + RW - 1) // RW * RW
    if n != flat.size:
        flat = np.concatenate([flat, np.zeros(n - flat.size, flat.dtype)])
    return flat.reshape(-1, RW)


class Sched:
    def __init__(s, nc, es):
        s.nc = nc
        s.eng = dict(pe=nc.tensor, dve=nc.vector, act=nc.scalar, pool=nc.gpsimd, sp=nc.sync)
        s.semobj = {}
        s.cnt = {}
        for k in ("pe", "dve", "act", "pool"):
            s.semobj[k] = es.enter_context(nc.semaphore("c_" + k))
            s.cnt[k] = 0
        s.dcnt = [0] * NDS
        for i in range(NDS):
            s.semobj[("d", i)] = es.enter_context(nc.semaphore("d%d" % i))
        s.dnext = 0
        s.bar = es.enter_context(nc.semaphore("bar"))
        s.semobj["bar"] = s.bar
        s.nbar = 0
        s.waited = {}
        s.res = {}
        s.nsem = 0
        s.es = es
        s.wcast = {}

    def newsem(s, name):
        sem = s.es.enter_context(s.nc.semaphore(name))
        s.semobj[name] = sem
        return name

    def _wait(s, e, ev):
        if ev is None:
            return
        sid, val = ev
        if sid == e and e == "pe":
            return
        if s.waited.get((e, sid), 0) >= val:
            return
        s.eng[e].wait_ge(s.semobj[sid], val)
        s.waited[(e, sid)] = val

    @staticmethod
    def _is_ps(k):
        return isinstance(k, tuple) and k[0] in ("psA", "psT", "psF")

    def _deps(s, e, reads, writes):
        for r in reads:
            st = s.res.get(r)
            if st:
                if s._is_ps(r):
                    if st[0] is not None and st[0][0] != e:
                        s._wait(e, st[0])
                    for sid, val in st[1].items():
                        if sid != e:
                            s._wait(e, (sid, val))
                else:
                    s._wait(e, st[0])
        for w in writes:
            st = s.res.get(w)
            if st:
                if s._is_ps(w):
                    if st[0] is not None and st[0][0] != e:
                        s._wait(e, st[0])
                    for sid, val in st[1].items():
                        if sid != e:
                            s._wait(e, (sid, val))
                else:
                    s._wait(e, st[0])
                    for sid, val in st[1].items():
                        s._wait(e, (sid, val))

    def _commit(s, ev, reads, writes):
        for r in reads:
            st = s.res.setdefault(r, [None, {}])
            if st[1].get(ev[0], 0) < ev[1]:
                st[1][ev[0]] = ev[1]
        for w in writes:
            s.res[w] = [ev, {}]

    def op(s, e, fn, reads=(), writes=()):
        s._deps(e, reads, writes)
        ins = fn(s.eng[e])
        s.cnt[e] += 1
        ins.then_inc(s.semobj[e], 1)
        s._commit((e, s.cnt[e]), reads, writes)

    def dma(s, out, in_, reads=(), writes=()):
        q = "sp"
        i = s.dnext
        s.dnext = (i + 1) % NDS
        if s.dcnt[i]:
            s._wait(q, (("d", i), 16 * s.dcnt[i]))
        s._deps(q, reads, writes)
        s.eng[q].dma_start(out=out, in_=in_).then_inc(s.semobj[("d", i)], 16)
        s.dcnt[i] += 1
        s._commit((("d", i), 16 * s.dcnt[i]), reads, writes)

    def ext_event(s, res, ev):
        s.res[res] = [ev, {}]

    def barrier(s, scratch_a, scratch_b):
        for k in ("pe", "dve", "act", "pool"):
            if s.cnt[k]:
                s._wait("sp", (k, s.cnt[k]))
        for i in range(NDS):
            if s.dcnt[i]:
                s._wait("sp", (("d", i), 16 * s.dcnt[i]))
        s.nbar += 1
        s.eng["sp"].dma_start(out=scratch_a, in_=scratch_b).then_inc(s.bar, 16)
        for k in ("pe", "dve", "act", "pool", "sp"):
            s._wait(k, ("bar", 16 * s.nbar))
        s.res = {}


def build_program(cfg):
    c = cfg
    D, TOK, KC, DI, NH, G, GW, HPG, CONVD = c.D, c.TOK, c.KC, c.DI, c.NH, c.G, c.GW, c.HPG, c.CONVD
    NB = TOK // TB
    NTT = TB // 128
    n_ssd = (c.DEPTH + 1) // 2
    n_sc = c.DEPTH // 2
    nc = bass.Bass("TRN2", target_bir_lowering=False)
    es = ExitStack()
    S = Sched(nc, es)

    def dram_in(name, shape, dt=F32):
        return nc.dram_tensor(name, list(shape), dt, kind="ExternalInput").ap()

    x_in = dram_in("x", [TOK, D])
    consts_in = dram_in("consts", [128, 4 * 128])
    y_out = nc.dram_tensor("y", [TOK, D], F32, kind="ExternalOutput").ap()

    blob_shapes = {}

    def blob_rows(nelem):
        return (nelem + RW - 1) // RW

    wspecs = []
    for j in range(n_ssd):
        wspecs += [("ssd_wz%d" % j, D * DI), ("ssd_wx%d" % j, D * CONVD), ("ssd_wdt%d" % j, D * NH),
                   ("ssd_wo%d" % j, DI * D)]
    for j in range(n_sc):
        wspecs += [("sc_wi%d" % j, D * 3 * D), ("sc_wo%d" % j, D * D)]
    for i in range(c.DEPTH):
        wspecs += [("wq%d" % i, D * c.QD), ("ut%d" % i, D * c.E), ("v%d" % i, c.E * D)]
    win, wbounce, wg = {}, {}, {}
    for name, nelem in wspecs:
        r = blob_rows(nelem)
        win[name] = dram_in(name, [r, RW])
        wg[name] = nc.dram_tensor("wg_" + name, [r, RW], BF16)

    sk_in = dram_in("skT", [c.DEPTH, 128, 16 * 128])
    ln_in = dram_in("lnp", [c.DEPTH, 4, D])
    if n_ssd:
        ssd_cw_in = dram_in("ssd_cw", [n_ssd, 128, (CONVD // 128) * 5])
        ssd_hp_in = dram_in("ssd_hp", [n_ssd, 3, NH])
        ssd_nw_in = dram_in("ssd_nw", [n_ssd, DI])
    if n_sc:
        sc_cw_in = dram_in("sc_cw", [n_sc, 128, KC * 3])

    XR = nc.dram_tensor("XR", [TOK, D], F32).ap()
    XT = nc.dram_tensor("XT", [128, KC, TOK], BF16).ap()
    DELTA = nc.dram_tensor("DELTA", [TOK, D], F32).ap()
    if n_ssd:
        XS = nc.dram_tensor("XS", [TOK, DI], F32).ap()
        ZS = nc.dram_tensor("ZS", [TOK, DI], F32).ap()
        BTd = nc.dram_tensor("BTd", [G, 128, TOK], BF16).ap()
        CTd = nc.dram_tensor("CTd", [G, 128, TOK], BF16).ap()
        BTM = nc.dram_tensor("BTM", [TOK, G * 128], BF16).ap()
        DTd = nc.dram_tensor("DTd", [TOK, NH], F32).ap()
        DAd = nc.dram_tensor("DAd", [TOK, NH], F32).ap()
        HT = nc.dram_tensor("HT", [128, DI // 128, TOK], BF16).ap()
    BAR = nc.dram_tensor("BARS", [2, 64], F32).ap()

    psA = [es.enter_context(nc.psum_tensor("psA%d" % i, [128, 512], F32)) for i in range(6)]
    psT = es.enter_context(nc.psum_tensor("psT", [128, 1024], BF16))
    psF = es.enter_context(nc.psum_tensor("psF", [128, 512], F32))

    uid = [0]

    def sb(name, shape, dt=F32, stack=None):
        uid[0] += 1
        return (stack or es).enter_context(nc.sbuf_tensor("%s_%d" % (name, uid[0]), list(shape), dt))

    cst = sb("cst", [128, 512])
    ident = cst[:, 0:128]
    tri = cst[:, 128:256]
    Umat = cst[:, 256:384]
    ones = cst[:, 384:512]
    identb = sb("identb", [128, 128], BF16)
    one_col = sb("one_col", [128, 1])
    eps_col = sb("eps_col", [128, 1])

    S.dma(cst[:], consts_in, writes=["cst"])
    S.op("dve", lambda e: e.tensor_copy(out=identb[:], in_=ident), reads=["cst"], writes=["identb"])
    S.op("dve", lambda e: e.memset(one_col[:], 1.0), writes=["one_col"])
    S.op("dve", lambda e: e.memset(eps_col[:], 1e-5), writes=["eps_col"])

    g = nc.gpsimd
    CH = 1024
    import os as _os
    for name, nelem in wspecs:
        r = win[name].shape[0]
        S.newsem("ag_" + name)
        S.wcast[name] = 16 * ((r + CH - 1) // CH)

    def layer_weights(layer):
        j = layer // 2
        if layer % 2 == 0:
            names = ["ssd_wx%d" % j, "ssd_wz%d" % j, "ssd_wdt%d" % j, "ssd_wo%d" % j]
        else:
            names = ["sc_wi%d" % j, "sc_wo%d" % j]
        return names + ["wq%d" % layer, "ut%d" % layer, "v%d" % layer]

    def issue_casts(layer):
        if layer >= c.DEPTH:
            return
        for name in layer_weights(layer):
            r = win[name].shape[0]
            for r0 in range(0, r, CH):
                r1 = min(r, r0 + CH)
                g.dma_start(out=wg[name][r0:r1, :], in_=win[name][r0:r1, :]).then_inc(S.semobj["ag_" + name], 16)
            S.ext_event("W:" + name, ("ag_" + name, S.wcast[name]))
            cast_done.add(name)

    cast_done = set()
    issue_casts(0)

    def wflat(name):
        return wg[name].ap().rearrange("r w -> (r w)")

    def wtile(name, off, x):
        return wflat(name)[off:off + 128 * x].rearrange("(p x) -> p x", p=128)

    scratch_sb = sb("bar_sb", [2, 64])

    def barrier():
        S.barrier(scratch_sb[0:1, :], BAR[1:2, :])
        for name in cast_done:
            S.ext_event("W:" + name, ("ag_" + name, S.wcast[name]))

    rr = {"fm": 0, "tm": 0, "t16": 0, "t32": 0}
    dst_keys = {}

    def load_xT(dst, blk, src=None, kc=None):
        src = XT if src is None else src
        kc = KC if kc is None else kc
        t0 = blk * TB
        if "L" in _os.environ.get("KB", ""):
            return
        d3 = dst[:].rearrange("p (k t) -> p k t", k=kc)
        for k0 in range(0, kc, 16):
            k1 = min(kc, k0 + 16)
            S.dma(d3[:, k0:k1, :], src[:, k0:k1, t0:t0 + TB], writes=[(dst.name, k0)])
        dst_keys[dst.name] = [(dst.name, k0) for k0 in range(0, kc, 16)]

    def fm_gemm(wname, fc_list, actT, kc, wpool, evac, base_off=0):
        nw = len(wpool)

        def load(i):
            fc = fc_list[i]
            wt = wpool[i % nw]
            S.dma(wt[:], wtile(wname, base_off + fc * 128 * kc * 128, kc * 128),
                  reads=["W:" + wname], writes=[wt.name])

        n = len(fc_list)
        la = nw - 1
        for i in range(min(la, n)):
            load(i)
        for i in range(n):
            if i + la < n:
                load(i + la)
            wt = wpool[i % nw]
            j = rr["fm"] % 4
            rr["fm"] += 1
            ps = psA[j][:, 0:TB]
            psn = ("psA", j)
            for k in range(kc):
                S.op("pe", lambda e, k=k: e.matmul(ps, lhsT=wt[:, k * 128:(k + 1) * 128],
                                                    rhs=actT[:, k * TB:(k + 1) * TB],
                                                    start=(k == 0), stop=(k == kc - 1)),
                     reads=[wt.name] + dst_keys.get(actT.name, [actT.name]), writes=[psn])
            evac(i, fc_list[i], ps, psn)

    def tm_gemm(wname, K, F, FW, actT, wpool, evac, base_off=0):
        kc = K // 128
        kgs = kgs_of(K)
        KG = kc // kgs
        FQ = F // FW
        nw = len(wpool)
        items = [(fq, kg) for fq in range(FQ) for kg in range(KG)]

        def load(i):
            fq, kg = items[i]
            wt = wpool[i % nw]
            S.dma(wt[:, 0:kgs * FW], wtile(wname, base_off + (fq * KG + kg) * 128 * kgs * FW, kgs * FW),
                  reads=["W:" + wname], writes=[wt.name])

        n = len(items)
        la = nw - 1
        for i in range(min(la, n)):
            load(i)
        for i in range(n):
            if i + la < n:
                load(i + la)
            fq, kg = items[i]
            wt = wpool[i % nw]
            if kg == 0:
                pp = rr["tm"] % 2
                rr["tm"] += 1
            for tt in range(NTT):
                ps = psA[pp * 2 + tt][:, 0:FW]
                psn = ("psA", pp * 2 + tt)
                for k in range(kgs):
                    kk = kg * kgs + k
                    S.op("pe", lambda e, k=k, kk=kk, tt=tt, ps=ps: e.matmul(
                        ps, lhsT=actT[:, kk * TB + tt * 128: kk * TB + tt * 128 + 128],
                        rhs=wt[:, k * FW:(k + 1) * FW], start=(kk == 0), stop=(kk == kc - 1)),
                        reads=[wt.name] + dst_keys.get(actT.name, [actT.name]), writes=[psn])
                if kg == KG - 1:
                    evac(fq, tt, ps, psn)

    def tr_group(srcs, src_res, dst_ap, dst_res, bank, bankname, kind, eng="dve"):
        n = len(srcs)
        for i, sa in enumerate(srcs):
            o = bank[:, i * 128:(i + 1) * 128]
            if kind == "mm":
                S.op("pe", lambda e, sa=sa, o=o: e.matmul(o, lhsT=sa, rhs=identb[:], start=True, stop=True),
                     reads=[src_res, "identb"], writes=[bankname])
            elif kind == "tf":
                S.op("pe", lambda e, sa=sa, o=o: e.transpose(out=o, in_=sa, identity=ident),
                     reads=[src_res, "cst"], writes=[bankname])
            else:
                S.op("pe", lambda e, sa=sa, o=o: e.transpose(out=o, in_=sa, identity=identb[:]),
                     reads=[src_res, "identb"], writes=[bankname])
        if eng == "act":
            S.op("act", lambda e: e.activation(out=dst_ap, in_=bank[:, 0:n * 128], func=AF.Copy),
                 reads=[bankname], writes=[dst_res])
        else:
            S.op(eng, lambda e: e.tensor_copy(out=dst_ap, in_=bank[:, 0:n * 128]), reads=[bankname], writes=[dst_res])

    def store_xT_tile(xn_tile, tok0, st):
        xb = st["xb"]
        xo = st["xo"]
        S.op("act", lambda e: e.activation(out=xb[:], in_=xn_tile[:], func=AF.Copy),
             reads=[xn_tile.name], writes=[xb.name])
        for g0 in range(0, KC, 4):
            n = min(4, KC - g0)
            j = rr["t16"] % 4
            rr["t16"] += 1
            tr_group([xb[:, (g0 + i) * 128:(g0 + i + 1) * 128] for i in range(n)], xb.name,
                     xo[:, g0 * 128:(g0 + n) * 128], (xo.name, g0), psA[j], ("psA", j), "mm",
                     eng=("dve" if (g0 // 4) % 2 == 0 else "act"))
        xo3 = xo[:].rearrange("p (k t) -> p k t", k=KC)
        for k0 in range(0, KC, 16):
            k1 = min(KC, k0 + 16)
            S.dma(XT[:, k0:k1, tok0:tok0 + 128], xo3[:, k0:k1, :],
                  reads=[(xo.name, g0) for g0 in range(k0, k1, 4)], writes=[])

    def stage_init():
        with ExitStack() as st_:
            xt = [sb("i_x%d" % i, [128, D], F32, st_) for i in range(2)]
            st = {"xb": sb("i_xb", [128, D], BF16, st_), "xo": sb("i_xo", [128, D], BF16, st_)}
            for t in range(TOK // 128):
                x_ = xt[t % 2]
                S.dma(x_[:], x_in[t * 128:(t + 1) * 128, :], writes=[x_.name])
                if "X" not in _os.environ.get("KB", ""):
                    S.dma(XR[t * 128:(t + 1) * 128, :], x_[:], reads=[x_.name], writes=[])
                if "T" not in _os.environ.get("KB", ""):
                    store_xT_tile(x_, t * 128, st)
            barrier()

    def stage_ln_(blk, layer, which, final):
        with ExitStack() as st_:
            xr = [sb("l_xr%d" % i, [128, D], F32, st_) for i in range(2)]
            dl = [sb("l_dl%d" % i, [128, D], F32, st_) for i in range(2)]
            grep = sb("l_g", [128, D], F32, st_)
            brep = sb("l_b", [128, D], F32, st_)
            stats = sb("l_st", [128, (D // 512) * 6], F32, st_)
            mv = sb("l_mv", [128, 2], F32, st_)
            rstd = sb("l_rs", [128, 1], F32, st_)
            st = {"xb": sb("l_xb", [128, D], BF16, st_), "xo": sb("l_xo", [128, D], BF16, st_)}
            S.dma(grep[:], ln_in[layer, 2 * which:2 * which + 1, :].to_broadcast([128, D]), writes=[grep.name])
            S.dma(brep[:], ln_in[layer, 2 * which + 1:2 * which + 2, :].to_broadcast([128, D]), writes=[brep.name])
            for tt in range(NTT):
                t = blk * NTT + tt
                x_, d_ = xr[tt % 2], dl[tt % 2]
                S.dma(x_[:], XR[t * 128:(t + 1) * 128, :], reads=[], writes=[x_.name])
                S.dma(d_[:], DELTA[t * 128:(t + 1) * 128, :], reads=[], writes=[d_.name])
                S.op("dve", lambda e: e.scalar_tensor_tensor(out=d_[:], in0=x_[:], scalar=float(c.alpha),
                                                             in1=d_[:], op0=ALU.mult, op1=ALU.add),
                     reads=[x_.name, d_.name], writes=[d_.name])
                for q in range(D // 512):
                    S.op("dve", lambda e, q=q: e.bn_stats(out=stats[:, q * 6:(q + 1) * 6],
                                                         in_=d_[:, q * 512:(q + 1) * 512]),
                         reads=[d_.name], writes=[(stats.name, q)])
                S.op("dve", lambda e: e.bn_aggr(out=mv[:], in_=stats[:]),
                     reads=[(stats.name, q) for q in range(D // 512)], writes=[mv.name])
                S.op("act", lambda e: e.activation(out=rstd[:], in_=mv[:, 1:2], func=AF.Ln, bias=eps_col[:], scale=1.0),
                     reads=[mv.name, "eps_col"], writes=[rstd.name])
                S.op("act", lambda e: e.activation(out=rstd[:], in_=rstd[:], func=AF.Exp, scale=-0.5),
                     reads=[rstd.name], writes=[rstd.name])
                S.op("dve", lambda e: e.tensor_scalar(out=d_[:], in0=d_[:], scalar1=mv[:, 0:1], scalar2=rstd[:],
                                                      op0=ALU.subtract, op1=ALU.mult),
                     reads=[d_.name, mv.name, rstd.name], writes=[d_.name])
                S.op("pool", lambda e: e.tensor_tensor(out=d_[:], in0=d_[:], in1=grep[:], op=ALU.mult),
                     reads=[d_.name, grep.name], writes=[d_.name])
                S.op("pool", lambda e: e.tensor_tensor(out=x_[:], in0=d_[:], in1=brep[:], op=ALU.add),
                     reads=[d_.name, brep.name], writes=[x_.name])
                if final:
                    S.dma(y_out[t * 128:(t + 1) * 128, :], x_[:], reads=[x_.name], writes=[])
                else:
                    S.dma(XR[t * 128:(t + 1) * 128, :], x_[:], reads=[x_.name], writes=[])
                    store_xT_tile(x_, t * 128, st)
            barrier()

    def stage_sc_(blk, j, carry, cw):
        with ExitStack() as st_:
            xT = sb("s_xT", [128, KC * TB], BF16, st_)
            yT = sb("s_yT", [128, KC * TB], BF16, st_)
            wpool = [sb("s_w%d" % i, [128, KC * 128], BF16, st_) for i in range(6)]
            wpool2 = [sb("s_v%d" % i, [128, kgs_of(D) * 512], BF16, st_) for i in range(4)]
            cs = [sb("s_cs%d" % i, [128, TB], F32, st_) for i in range(2)]
            ub = [sb("s_ub%d" % i, [128, TB + 2], F32, st_) for i in range(2)]
            acc = [sb("s_ac%d" % i, [128, TB], F32, st_) for i in range(2)]
            stg = [sb("s_sg%d" % i, [128, 512], F32, st_) for i in range(4)]
            load_xT(xT, blk)
            wn = "sc_wi%d" % j
            order = []
            for fc in range(KC):
                order += [KC + fc, 2 * KC + fc, fc]

            def evac(i, fcw, ps, psn):
                fc = fcw % KC
                kind = fcw // KC
                p = fc % 2
                if kind == 1:
                    S.op("act", lambda e: e.activation(out=cs[p][:], in_=ps, func=AF.Copy),
                         reads=[psn], writes=[cs[p].name])
                elif kind == 2:
                    S.op("dve", lambda e: e.tensor_copy(out=ub[p][:, 0:2], in_=carry[:, 2 * fc:2 * fc + 2]),
                         reads=[("carry", fc)], writes=[(ub[p].name, "h")])
                    S.op("dve", lambda e: e.tensor_tensor(out=ub[p][:, 2:2 + TB], in0=cs[p][:], in1=ps, op=ALU.mult),
                         reads=[cs[p].name, psn], writes=[(ub[p].name, "b")])
                    S.op("dve", lambda e: e.tensor_copy(out=carry[:, 2 * fc:2 * fc + 2], in_=ub[p][:, TB:TB + 2]),
                         reads=[(ub[p].name, "b")], writes=[("carry", fc)])
                    rd = [(ub[p].name, "h"), (ub[p].name, "b"), "sc_cw"]
                    S.op("dve", lambda e: e.tensor_scalar(out=acc[p][:], in0=ub[p][:, 0:TB],
                                                           scalar1=cw[:, 3 * fc:3 * fc + 1], scalar2=None,
                                                           op0=ALU.mult),
                         reads=rd, writes=[acc[p].name])
                    for kk in (1, 2):
                        S.op("dve", lambda e, kk=kk: e.scalar_tensor_tensor(
                            out=acc[p][:], in0=ub[p][:, kk:kk + TB], scalar=cw[:, 3 * fc + kk:3 * fc + kk + 1],
                            in1=acc[p][:], op0=ALU.mult, op1=ALU.add),
                            reads=rd + [acc[p].name], writes=[acc[p].name])
                else:
                    S.op("dve", lambda e: e.tensor_tensor(out=yT[:, fc * TB:(fc + 1) * TB], in0=acc[p][:], in1=ps,
                                                          op=ALU.mult),
                         reads=[acc[p].name, psn], writes=[yT.name])

            fm_gemm(wn, order, xT, KC, wpool, evac)

            def evac2(fq, tt, ps, psn):
                sgi = stg[(fq * NTT + tt) % 4]
                S.op("act", lambda e: e.activation(out=sgi[:], in_=ps, func=AF.Copy), reads=[psn], writes=[sgi.name])
                t = blk * NTT + tt
                S.dma(DELTA[t * 128:(t + 1) * 128, fq * 512:(fq + 1) * 512], sgi[:], reads=[sgi.name],
                      writes=[])

            tm_gemm("sc_wo%d" % j, D, D, 512, yT, wpool2, evac2)
            barrier()

    NXC = CONVD // 128
    NXS = DI // 128

    def stage_ssd1_(blk, j, carry, cw, hp):
        with ExitStack() as st_:
            xT = sb("a_xT", [128, KC * TB], BF16, st_)
            wpool = [sb("a_w%d" % i, [128, KC * 128], BF16, st_) for i in range(6)]
            wpool2 = [sb("a_v%d" % i, [128, kgs_of(D) * 512], BF16, st_) for i in range(4)]
            cb = [sb("a_cb%d" % i, [128, TB + 3], F32, st_) for i in range(2)]
            acc = [sb("a_ac%d" % i, [128, TB], F32, st_) for i in range(2)]
            res = [sb("a_rs%d" % i, [128, TB], F32, st_) for i in range(2)]
            resb = [sb("a_rb%d" % i, [128, TB], BF16, st_) for i in range(2)]
            stg = [sb("a_sg%d" % i, [128, 512], F32, st_) for i in range(4)]
            stgb = [sb("a_sb%d" % i, [128, NTT * 128], BF16, st_) for i in range(2)]
            dtt = [sb("a_dt%d" % i, [128, NH], F32, st_) for i in range(2)]
            dta = [sb("a_da%d" % i, [128, NH], F32, st_) for i in range(2)]
            load_xT(xT, blk)
            t0 = blk * TB

            def evac(i, fc, ps, psn):
                p = i % 2
                S.op("dve", lambda e: e.tensor_copy(out=cb[p][:, 0:3], in_=carry[:, 3 * fc:3 * fc + 3]),
                     reads=[("carry", fc)], writes=[(cb[p].name, "h")])
                S.op("act", lambda e: e.activation(out=cb[p][:, 3:3 + TB], in_=ps, func=AF.Copy),
                     reads=[psn], writes=[(cb[p].name, "b")])
                S.op("dve", lambda e: e.tensor_copy(out=carry[:, 3 * fc:3 * fc + 3], in_=cb[p][:, TB:TB + 3]),
                     reads=[(cb[p].name, "b")], writes=[("carry", fc)])
                rd = [(cb[p].name, "h"), (cb[p].name, "b"), "ssd_cw"]
                S.op("dve", lambda e: e.tensor_scalar(out=acc[p][:], in0=cb[p][:, 0:TB],
                                                       scalar1=cw[:, 5 * fc:5 * fc + 1], scalar2=None, op0=ALU.mult),
                     reads=rd, writes=[acc[p].name])
                for kk in (1, 2, 3):
                    S.op("dve", lambda e, kk=kk: e.scalar_tensor_tensor(
                        out=acc[p][:], in0=cb[p][:, kk:kk + TB], scalar=cw[:, 5 * fc + kk:5 * fc + kk + 1],
                        in1=acc[p][:], op0=ALU.mult, op1=ALU.add),
                        reads=rd + [acc[p].name], writes=[acc[p].name])
                S.op("act", lambda e: e.activation(out=res[p][:], in_=acc[p][:], func=AF.Silu,
                                                   bias=cw[:, 5 * fc + 4:5 * fc + 5], scale=1.0),
                     reads=[acc[p].name, "ssd_cw"], writes=[res[p].name])
                bk = 4 + (i % 2)
                if fc < NXS:
                    sg = stg[i % 4]
                    tr_group([res[p][:, tt * 128:(tt + 1) * 128] for tt in range(NTT)], res[p].name,
                             sg[:, 0:NTT * 128], sg.name, psA[bk], ("psA", bk), "tf")
                    for tt in range(NTT):
                        S.dma(XS[t0 + tt * 128:t0 + (tt + 1) * 128, fc * 128:(fc + 1) * 128],
                              sg[:, tt * 128:(tt + 1) * 128], reads=[sg.name], writes=[])
                else:
                    gi = (fc - NXS) % G
                    isB = (fc - NXS) < G
                    S.op("pool", lambda e: e.tensor_copy(out=resb[p][:], in_=res[p][:]),
                         reads=[res[p].name], writes=[resb[p].name])
                    S.dma((BTd if isB else CTd)[gi, :, t0:t0 + TB], resb[p][:], reads=[resb[p].name],
                          writes=[])
                    if isB:
                        sg = stgb[i % 2]
                        tr_group([resb[p][:, tt * 128:(tt + 1) * 128] for tt in range(NTT)], resb[p].name,
                                 sg[:, 0:NTT * 128], sg.name, psA[bk], ("psA", bk), "mm")
                        for tt in range(NTT):
                            S.dma(BTM[t0 + tt * 128:t0 + (tt + 1) * 128, gi * 128:(gi + 1) * 128],
                                  sg[:, tt * 128:(tt + 1) * 128], reads=[sg.name], writes=[])

            if "a" not in _os.environ.get("KB", ""):
                fm_gemm("ssd_wx%d" % j, list(range(NXC)), xT, KC, wpool, evac)

            def evac_z(fq, tt, ps, psn):
                sg = stg[(fq * NTT + tt) % 4]
                S.op("act", lambda e: e.activation(out=sg[:], in_=ps, func=AF.Silu), reads=[psn], writes=[sg.name])
                S.dma(ZS[t0 + tt * 128:t0 + (tt + 1) * 128, fq * 512:(fq + 1) * 512], sg[:], reads=[sg.name],
                      writes=[])

            if "z" not in _os.environ.get("KB", ""):
                tm_gemm("ssd_wz%d" % j, D, DI, 512, xT, wpool2, evac_z)

            def evac_dt(fq, tt, ps, psn):
                d_, a_ = dtt[tt % 2], dta[tt % 2]
                S.op("dve", lambda e: e.tensor_tensor(out=d_[:], in0=ps, in1=hp[:, 0:NH], op=ALU.add),
                     reads=[psn, "ssd_hp"], writes=[d_.name])
                S.op("act", lambda e: e.activation(out=d_[:], in_=d_[:], func=AF.Exp), reads=[d_.name], writes=[d_.name])
                S.op("act", lambda e: e.activation(out=d_[:], in_=d_[:], func=AF.Ln, bias=one_col[:], scale=1.0),
                     reads=[d_.name, "one_col"], writes=[d_.name])
                S.op("dve", lambda e: e.tensor_tensor(out=a_[:], in0=d_[:], in1=hp[:, NH:2 * NH], op=ALU.mult),
                     reads=[d_.name, "ssd_hp"], writes=[a_.name])
                S.dma(DTd[t0 + tt * 128:t0 + (tt + 1) * 128, :], d_[:], reads=[d_.name], writes=[])
                S.dma(DAd[t0 + tt * 128:t0 + (tt + 1) * 128, :], a_[:], reads=[a_.name], writes=[])

            if "d" not in _os.environ.get("KB", ""):
                tm_gemm("ssd_wdt%d" % j, D, NH, NH, xT, wpool2, evac_dt)
            barrier()

    HQ = min(4, HPG)

    def stage_ssd2_(ch, j, ST32, STb, hp, nwrep, bufs):
        t0 = ch * 128
        b = bufs
        dt_, da_ = b["dt"], b["da"]
        S.dma(dt_[:], DTd[t0:t0 + 128, :], reads=[], writes=[dt_.name])
        S.dma(da_[:], DAd[t0:t0 + 128, :], reads=[], writes=[da_.name])
        eacs, eend, etot = b["eacs"], b["eend"], b["etot"]
        for (lh, dst, jx) in ((tri, eacs, 0), (Umat, eend, 1), (ones, etot, 2)):
            ps = psF[:, jx * 128:jx * 128 + NH]
            S.op("pe", lambda e, lh=lh, ps=ps: e.matmul(ps, lhsT=lh, rhs=da_[:], start=True, stop=True),
                 reads=["cst", da_.name], writes=[("psF", 0)])
        for (lh, dst, jx) in ((tri, eacs, 0), (Umat, eend, 1), (ones, etot, 2)):
            ps = psF[:, jx * 128:jx * 128 + NH]
            S.op("act", lambda e, ps=ps, dst=dst: e.activation(out=dst[:], in_=ps, func=AF.Exp),
                 reads=[("psF", 0)], writes=[dst.name])
        for gi in range(G):
            p = gi % 2
            xs, zs, bt, ct, btm = b["xs"][p], b["zs"][p], b["bt"][p], b["ct"][p], b["btm"][p]
            S.dma(xs[:], XS[t0:t0 + 128, gi * GW:(gi + 1) * GW], reads=[], writes=[xs.name])
            S.dma(zs[:], ZS[t0:t0 + 128, gi * GW:(gi + 1) * GW], reads=[], writes=[zs.name])
            S.dma(bt[:], BTd[gi, :, t0:t0 + 128], reads=[], writes=[bt.name])
            S.dma(ct[:], CTd[gi, :, t0:t0 + 128], reads=[], writes=[ct.name])
            S.dma(btm[:], BTM[t0:t0 + 128, gi * 128:(gi + 1) * 128], reads=[], writes=[btm.name])
            h0 = gi * HPG
            xdt, xdtw = b["xdt"][p], b["xdtw"][p]
            v3 = lambda t_: t_[:].rearrange("p (j d) -> p j d", j=HPG)
            bc = lambda col: col[:, h0:h0 + HPG].unsqueeze(2).to_broadcast([128, HPG, c.HD])
            S.op("dve", lambda e: e.tensor_tensor(out=v3(xdt), in0=v3(xs), in1=bc(dt_), op=ALU.mult),
                 reads=[xs.name, dt_.name], writes=[xdt.name])
            S.op("pool", lambda e: e.tensor_tensor(out=v3(xdtw), in0=v3(xdt), in1=bc(eend), op=ALU.mult),
                 reads=[xdt.name, eend.name], writes=[xdtw.name])
            cbm = b["cbm"][p]
            S.op("pe", lambda e: e.matmul(psA[4][:, 0:128], lhsT=bt[:], rhs=ct[:], start=True, stop=True),
                 reads=[bt.name, ct.name], writes=[("psA", 4)])
            S.op("dve", lambda e: e.tensor_tensor(out=cbm[:], in0=psA[4][:, 0:128], in1=tri, op=ALU.mult),
                 reads=[("psA", 4), "cst"], writes=[cbm.name])
            R = b["R"][p]
            S.op("pool", lambda e: e.tensor_tensor(
                out=R[:].rearrange("p (j l) -> p j l", j=HPG),
                in0=da_[:, h0:h0 + HPG].unsqueeze(2).to_broadcast([128, HPG, 128]),
                in1=tri.unsqueeze(1).to_broadcast([128, HPG, 128]), op=ALU.mult),
                reads=[da_.name, "cst"], writes=[R.name])
            nyb = (GW + 511) // 512
            for hq in range(HPG // HQ):
                dec, wT = b["dec"][hq % 2], b["wT"][hq % 2]
                W_ = HQ * 128
                S.op("pe", lambda e, hq=hq: e.matmul(psA[5][:, 0:W_], lhsT=Umat, rhs=R[:, hq * W_:(hq + 1) * W_],
                                                      start=True, stop=True),
                     reads=["cst", R.name], writes=[("psA", 5)])
                S.op("act", lambda e: e.activation(out=dec[:, 0:W_], in_=psA[5][:, 0:W_], func=AF.Exp),
                     reads=[("psA", 5)], writes=[dec.name])
                S.op("dve", lambda e: e.tensor_tensor(
                    out=wT[:, 0:W_].rearrange("p (j l) -> p j l", j=HQ),
                    in0=dec[:, 0:W_].rearrange("p (j l) -> p j l", j=HQ),
                    in1=cbm[:].unsqueeze(1).to_broadcast([128, HQ, 128]), op=ALU.mult),
                    reads=[dec.name, cbm.name], writes=[wT.name])
                for jj in range(HQ):
                    hh = hq * HQ + jj
                    col = hh * c.HD
                    S.op("pe", lambda e, jj=jj, col=col: e.matmul(
                        psA[col // 512][:, col % 512:col % 512 + c.HD], lhsT=wT[:, jj * 128:(jj + 1) * 128],
                        rhs=xdt[:, col:col + c.HD], start=True, stop=True),
                        reads=[wT.name, xdt.name], writes=[("psA", col // 512)])
            for q in range(nyb):
                wq_ = min(512, GW - q * 512)
                S.op("pe", lambda e, q=q, wq_=wq_: e.matmul(psA[2 + q][:, 0:wq_], lhsT=ct[:],
                                                           rhs=STb[:, gi * GW + q * 512: gi * GW + q * 512 + wq_],
                                                           start=True, stop=True),
                     reads=[ct.name, ("STb", gi)], writes=[("psA", 2 + q)])
            y = b["y"][p]
            for q in range(nyb):
                wq_ = min(512, GW - q * 512)
                nh_ = wq_ // c.HD
                hs = h0 + q * (512 // c.HD)
                S.op("dve", lambda e, q=q, wq_=wq_, nh_=nh_, hs=hs: e.tensor_tensor(
                    out=y[:, q * 512:q * 512 + wq_].rearrange("p (j d) -> p j d", j=nh_),
                    in0=psA[2 + q][:, 0:wq_].rearrange("p (j d) -> p j d", j=nh_),
                    in1=eacs[:, hs:hs + nh_].unsqueeze(2).to_broadcast([128, nh_, c.HD]), op=ALU.mult),
                    reads=[("psA", 2 + q), eacs.name], writes=[(y.name, q)])
                S.op("dve", lambda e, q=q, wq_=wq_: e.tensor_tensor(
                    out=y[:, q * 512:q * 512 + wq_], in0=y[:, q * 512:q * 512 + wq_], in1=psA[q][:, 0:wq_],
                    op=ALU.add), reads=[(y.name, q), ("psA", q)], writes=[(y.name, q)])
            yr = [(y.name, q) for q in range(nyb)]
            tmp = b["tmp"][p]
            S.op("pool", lambda e: e.tensor_tensor(out=v3(tmp), in0=v3(xs), in1=bc(hp[:, 2 * NH:3 * NH]) if False else
                                                   hp[:, 2 * NH + h0:2 * NH + h0 + HPG].unsqueeze(2).to_broadcast(
                                                       [128, HPG, c.HD]), op=ALU.mult),
                 reads=[xs.name, "ssd_hp"], writes=[tmp.name])
            S.op("pool", lambda e: e.tensor_tensor(out=y[:], in0=y[:], in1=tmp[:], op=ALU.add),
                 reads=yr + [tmp.name], writes=yr)
            S.op("dve", lambda e: e.tensor_tensor(out=y[:], in0=y[:], in1=zs[:], op=ALU.mult),
                 reads=yr + [zs.name], writes=yr)
            ss, rs = b["ss"][p], b["rs"][p]
            S.op("dve", lambda e: e.memset(ss[:], 0.0), writes=[ss.name])
            S.op("act", lambda e: e.activation(out=tmp[:], in_=y[:], func=AF.Square, accum_out=ss[:]),
                 reads=yr + [ss.name], writes=[tmp.name, ss.name])
            S.op("act", lambda e: e.activation(out=rs[:], in_=ss[:], func=AF.Ln, bias=eps_col[:], scale=1.0 / GW),
                 reads=[ss.name, "eps_col"], writes=[rs.name])
            S.op("act", lambda e: e.activation(out=rs[:], in_=rs[:], func=AF.Exp, scale=-0.5),
                 reads=[rs.name], writes=[rs.name])
            hn = b["hn"][p]
            S.op("dve", lambda e: e.scalar_tensor_tensor(out=hn[:], in0=y[:], scalar=rs[:],
                                                         in1=nwrep[:, gi * GW:(gi + 1) * GW],
                                                         op0=ALU.mult, op1=ALU.mult),
                 reads=yr + [rs.name, "nwrep"], writes=[hn.name])
            ho = b["ho"][p]
            tr_group([hn[:, k * 128:(k + 1) * 128] for k in range(GW // 128)], hn.name, ho[:], ho.name,
                     psT, ("psT", 0), "tb", eng=("act" if gi % 2 else "dve"))
            S.dma(HT[:, gi * (GW // 128):(gi + 1) * (GW // 128), t0:t0 + 128],
                  ho[:].rearrange("p (k t) -> p k t", k=GW // 128), reads=[ho.name], writes=[])
            for q in range(nyb):
                wq_ = min(512, GW - q * 512)
                S.op("pe", lambda e, q=q, wq_=wq_: e.matmul(psA[2 + q][:, 0:wq_], lhsT=btm[:],
                                                           rhs=xdtw[:, q * 512:q * 512 + wq_], start=True, stop=True),
                     reads=[btm.name, xdtw.name], writes=[("psA", 2 + q)])
            sg = ST32[:, gi * GW:(gi + 1) * GW]
            S.op("pool", lambda e: e.tensor_tensor(
                out=sg.rearrange("p (j d) -> p j d", j=HPG), in0=sg.rearrange("p (j d) -> p j d", j=HPG),
                in1=etot[:, h0:h0 + HPG].unsqueeze(2).to_broadcast([128, HPG, c.HD]), op=ALU.mult),
                reads=[("ST32", gi), etot.name], writes=[("ST32", gi)])
            for q in range(nyb):
                wq_ = min(512, GW - q * 512)
                S.op("dve", lambda e, q=q, wq_=wq_: e.tensor_tensor(
                    out=sg[:, q * 512:q * 512 + wq_], in0=sg[:, q * 512:q * 512 + wq_], in1=psA[2 + q][:, 0:wq_],
                    op=ALU.add), reads=[("ST32", gi), ("psA", 2 + q)], writes=[("ST32", gi)])
            S.op("act", lambda e: e.activation(out=STb[:, gi * GW:(gi + 1) * GW], in_=sg, func=AF.Copy),
                 reads=[("ST32", gi)], writes=[("STb", gi)])

    def stage_ssd3_(blk, j):
        with ExitStack() as st_:
            kc = DI // 128
            hT = sb("c_hT", [128, kc * TB], BF16, st_)
            wpool2 = [sb("c_v%d" % i, [128, kgs_of(DI) * 512], BF16, st_) for i in range(5)]
            stg = [sb("c_sg%d" % i, [128, 512], F32, st_) for i in range(4)]
            load_xT(hT, blk, src=HT, kc=kc)

            def evac2(fq, tt, ps, psn):
                sgi = stg[(fq * NTT + tt) % 4]
                S.op("act", lambda e: e.activation(out=sgi[:], in_=ps, func=AF.Copy), reads=[psn], writes=[sgi.name])
                t = blk * NTT + tt
                S.dma(DELTA[t * 128:(t + 1) * 128, fq * 512:(fq + 1) * 512], sgi[:], reads=[sgi.name],
                      writes=[])

            tm_gemm("ssd_wo%d" % j, DI, D, 512, hT, wpool2, evac2)
            barrier()

    PH = c.PH

    def stage_peer_(blk, layer, skT):
        with ExitStack() as st_:
            xT = sb("p_xT", [128, KC * TB], BF16, st_)
            sc = [sb("p_sc%d" % i, [128, 16 * 128], F32, st_) for i in range(NTT)]
            A2 = [sb("p_A%d" % i, [128, 2 * PH], F32, st_) for i in range(NTT)]
            dg = [sb("p_dg%d" % i, [128, PH * 128], BF16, st_) for i in range(NTT)]
            load_xT(xT, blk)
            with ExitStack() as sa:
                qT = sb("p_qT", [128, 16 * TB], F32, sa)
                wpool = [sb("p_w%d" % i, [128, KC * 128], BF16, sa) for i in range(3)]
                t16 = sb("p_t16", [128, 16 * 16], F32, sa)
                wk = sb("p_wk", [128, 256], F32, sa)
                cand = sb("p_cand", [128, PH * 256], F32, sa)
                tc16 = sb("p_tc", [128, PH * 16], F32, sa)
                ex = sb("p_ex", [128, PH * 16], F32, sa)
                Z = sb("p_Z", [128, PH], F32, sa)
                cc = sb("p_cc", [128, PH], F32, sa)

                def evq(i, fc, ps, psn):
                    S.op("act", lambda e: e.activation(out=qT[:, fc * TB:(fc + 1) * TB], in_=ps, func=AF.Copy),
                         reads=[psn], writes=[(qT.name, fc)])

                fm_gemm("wq%d" % layer, list(range(16)), xT, KC, wpool, evq)
                for tt in range(NTT):
                    for q4 in range(4):
                        for u_ in range(4):
                            qc = q4 * 4 + u_
                            S.op("pe", lambda e, qc=qc, u_=u_, tt=tt: e.matmul(
                                psA[tt][:, u_ * 128:(u_ + 1) * 128],
                                lhsT=qT[:, qc * TB + tt * 128: qc * TB + tt * 128 + 128],
                                rhs=skT[:, qc * 128:(qc + 1) * 128], start=True, stop=True),
                                reads=[(qT.name, qc), "skT"], writes=[("psA", tt)])
                        S.op("act", lambda e, q4=q4, tt=tt: e.activation(out=sc[tt][:, q4 * 512:(q4 + 1) * 512],
                                                                         in_=psA[tt][:, :], func=AF.Copy),
                             reads=[("psA", tt)], writes=[sc[tt].name])
                    s_ = sc[tt]
                    for hi in range(16):
                        src = s_[:, hi * 128:(hi + 1) * 128]
                        S.op("dve", lambda e, hi=hi, src=src: e.max(out=t16[:, hi * 16:hi * 16 + 8], in_=src),
                             reads=[s_.name], writes=[(t16.name, hi)])
                        S.op("dve", lambda e, hi=hi, src=src: e.match_replace(
                            out=wk[:, 0:128], in_to_replace=t16[:, hi * 16:hi * 16 + 8], in_values=src,
                            imm_value=-1e30), reads=[s_.name, (t16.name, hi)], writes=[wk.name])
                        S.op("dve", lambda e, hi=hi: e.max(out=t16[:, hi * 16 + 8:hi * 16 + 16], in_=wk[:, 0:128]),
                             reads=[wk.name], writes=[(t16.name, hi)])
                    tr = [(t16.name, hi) for hi in range(16)]
                    tv = t16[:].rearrange("p (h i j) -> p h i j", h=PH, i=2)
                    S.op("dve", lambda e: e.tensor_tensor(
                        out=cand[:].rearrange("p (h a b) -> p h a b", h=PH, a=16),
                        in0=tv[:, :, 0, :].unsqueeze(3).to_broadcast([128, PH, 16, 16]),
                        in1=tv[:, :, 1, :].unsqueeze(2).to_broadcast([128, PH, 16, 16]), op=ALU.add),
                        reads=tr, writes=[cand.name])
                    for h in range(PH):
                        src = cand[:, h * 256:(h + 1) * 256]
                        S.op("dve", lambda e, h=h, src=src: e.max(out=tc16[:, h * 16:h * 16 + 8], in_=src),
                             reads=[cand.name], writes=[(tc16.name, h)])
                        S.op("dve", lambda e, h=h, src=src: e.match_replace(
                            out=wk[:, 0:256], in_to_replace=tc16[:, h * 16:h * 16 + 8], in_values=src,
                            imm_value=-1e30), reads=[cand.name, (tc16.name, h)], writes=[wk.name])
                        S.op("dve", lambda e, h=h: e.max(out=tc16[:, h * 16 + 8:h * 16 + 16], in_=wk[:, 0:256]),
                             reads=[wk.name], writes=[(tc16.name, h)])
                    tcr = [(tc16.name, h) for h in range(PH)]
                    tcv = tc16[:].rearrange("p (h k) -> p h k", h=PH)
                    S.op("dve", lambda e: e.tensor_tensor(
                        out=ex[:].rearrange("p (h k) -> p h k", h=PH), in0=tcv,
                        in1=tcv[:, :, 0:1].to_broadcast([128, PH, 16]), op=ALU.subtract),
                        reads=tcr, writes=[ex.name])
                    S.op("act", lambda e: e.activation(out=ex[:], in_=ex[:], func=AF.Exp), reads=[ex.name],
                         writes=[ex.name])
                    S.op("dve", lambda e: e.tensor_reduce(out=Z[:], in_=ex[:].rearrange("p (h k) -> p h k", h=PH),
                                                          axis=AX.X, op=ALU.add), reads=[ex.name], writes=[Z.name])
                    S.op("dve", lambda e: e.reciprocal(out=Z[:], in_=Z[:]), reads=[Z.name], writes=[Z.name])
                    S.op("dve", lambda e: e.tensor_tensor(
                        out=cc[:], in0=ex[:].rearrange("p (h k) -> p h k", h=PH)[:, :, 15], in1=Z[:], op=ALU.mult),
                        reads=[ex.name, Z.name], writes=[cc.name])
                    S.op("dve", lambda e, tt=tt: e.tensor_copy(out=A2[tt][:, 0:PH], in_=tcv[:, :, 15]),
                         reads=tcr, writes=[A2[tt].name])
                    S.op("dve", lambda e, tt=tt: e.tensor_scalar(out=A2[tt][:, PH:2 * PH], in0=tcv[:, :, 15],
                                                                 scalar1=-1.0, scalar2=None, op0=ALU.mult),
                         reads=tcr, writes=[A2[tt].name])
                    for h in range(PH):
                        S.op("dve", lambda e, h=h, tt=tt: e.tensor_scalar(
                            out=dg[tt][:, h * 128:(h + 1) * 128], in0=ident, scalar1=cc[:, h:h + 1], scalar2=None,
                            op0=ALU.mult), reads=["cst", cc.name], writes=[(dg[tt].name, h)])
                barrier()
            with ExitStack() as sh:
                HdT = sb("p_Hd", [128, 128 * TB], BF16, sh)
                with ExitStack() as sbk:
                    upool = [sb("p_u%d" % i, [128, KC * 128], BF16, sbk) for i in range(5)]
                    Sb = [sb("p_S%d" % i, [128, PH * 2 * 128], F32, sbk) for i in range(2)]
                    Eb = [sb("p_E%d" % i, [128, PH * 2 * 128], BF16, sbk) for i in range(2)]
                    Mb = [sb("p_M%d" % i, [128, PH * 2 * 128], BF16, sbk) for i in range(2)]
                    gel = [sb("p_g%d" % i, [128, TB], F32, sbk) for i in range(2)]
                    pre_ps = {}

                    def evu(i, k1, ps, psn):
                        gi_ = 4 + (k1 // 2) % 2
                        if k1 % 2 == 0:
                            for tt in range(NTT):
                                ib = ((k1 // 2) * NTT + tt) % 2
                                S_, E_, M_ = Sb[ib], Eb[ib], Mb[ib]
                                svw = sc[tt][:].rearrange("p (h i k) -> p h i k", h=PH, i=2)
                                S.op("pool", lambda e, tt=tt, S_=S_, svw=svw: e.tensor_tensor(
                                    out=S_[:].rearrange("p (h a k) -> p h a k", h=PH, a=2),
                                    in0=svw[:, :, 0, k1:k1 + 2].unsqueeze(3).to_broadcast([128, PH, 2, 128]),
                                    in1=svw[:, :, 1, :].unsqueeze(2).to_broadcast([128, PH, 2, 128]), op=ALU.add),
                                    reads=[sc[tt].name], writes=[S_.name])
                                for h in range(PH):
                                    S.op("act", lambda e, S_=S_, E_=E_, h=h, tt=tt: e.activation(
                                        out=E_[:, h * 256:(h + 1) * 256], in_=S_[:, h * 256:(h + 1) * 256],
                                        func=AF.Exp, bias=A2[tt][:, PH + h:PH + h + 1], scale=1.0),
                                        reads=[S_.name, A2[tt].name], writes=[(E_.name, h)])
                                    S.op("dve", lambda e, S_=S_, E_=E_, M_=M_, h=h, tt=tt: e.scalar_tensor_tensor(
                                        out=M_[:, h * 256:(h + 1) * 256], in0=S_[:, h * 256:(h + 1) * 256],
                                        scalar=A2[tt][:, h:h + 1], in1=E_[:, h * 256:(h + 1) * 256],
                                        op0=ALU.is_ge, op1=ALU.mult),
                                        reads=[S_.name, (E_.name, h), A2[tt].name], writes=[(M_.name, h)])
                                for a in range(2):
                                    for h in range(PH):
                                        o0 = (a * NTT + tt) * 128
                                        S.op("pe", lambda e, a=a, h=h, tt=tt, M_=M_, o0=o0: e.matmul(
                                            psA[gi_][:, o0:o0 + 128],
                                            lhsT=M_[:, (h * 2 + a) * 128:(h * 2 + a + 1) * 128],
                                            rhs=dg[tt][:, h * 128:(h + 1) * 128], start=(h == 0), stop=(h == PH - 1)),
                                            reads=[(M_.name, h), (dg[tt].name, h)],
                                            writes=[("psA", gi_)])
                        gl = gel[k1 % 2]
                        S.op("act", lambda e: e.activation(out=gl[:], in_=ps, func=AF.Gelu), reads=[psn],
                             writes=[gl.name])
                        a = k1 % 2
                        S.op("dve", lambda e: e.tensor_tensor(
                            out=HdT[:, k1 * TB:(k1 + 1) * TB], in0=gl[:],
                            in1=psA[gi_][:, a * TB:(a + 1) * TB], op=ALU.mult),
                            reads=[gl.name, ("psA", gi_)], writes=[HdT.name])

                    fm_gemm("ut%d" % layer, list(range(128)), xT, KC, upool, evu)
                    barrier()
                with ExitStack() as sc_:
                    vpool = [sb("p_v%d" % i, [128, 16 * 512], BF16, sc_) for i in range(4)]
                    stg = [sb("p_sg%d" % i, [128, 512], F32, sc_) for i in range(4)]

                    def evv(fq, tt, ps, psn):
                        sgi = stg[(fq * NTT + tt) % 4]
                        S.op("act", lambda e: e.activation(out=sgi[:], in_=ps, func=AF.Copy), reads=[psn],
                             writes=[sgi.name])
                        t = blk * NTT + tt
                        S.dma(DELTA[t * 128:(t + 1) * 128, fq * 512:(fq + 1) * 512], sgi[:], reads=[sgi.name],
                              writes=[])

                    tm_gemm("v%d" % layer, c.E, D, 512, HdT, vpool, evv)
                    barrier()

    def _forward():
        for layer in range(c.DEPTH):
            j = layer // 2
            issue_casts(layer + 1)
            if layer % 2 == 0:
                with ExitStack() as sl:
                    carry = sb("q_carry", [128, NXC * 3], F32, sl)
                    cw = sb("q_cw", [128, NXC * 5], F32, sl)
                    hp = sb("q_hp", [128, 3 * NH], F32, sl)
                    S.op("dve", lambda e: e.memset(carry[:], 0.0), writes=[("carry", fc) for fc in range(NXC)])
                    S.dma(cw[:], ssd_cw_in[j], writes=["ssd_cw"])
                    S.dma(hp[:], ssd_hp_in[j:j + 1].rearrange("o a n -> o (a n)").to_broadcast([128, 3 * NH]),
                          writes=["ssd_hp"])
                    S.op("act", lambda e: e.activation(out=hp[:, NH:2 * NH], in_=hp[:, NH:2 * NH], func=AF.Exp),
                         reads=["ssd_hp"], writes=["ssd_hp"])
                    S.op("dve", lambda e: e.tensor_scalar(out=hp[:, NH:2 * NH], in0=hp[:, NH:2 * NH], scalar1=-1.0,
                                                          scalar2=None, op0=ALU.mult), reads=["ssd_hp"], writes=["ssd_hp"])
                    barrier()
                    S.res["ssd_cw"] = [None, {}]
                    for blk in range(NB):
                        stage_ssd1(blk, j, carry, cw, hp)
                    with ExitStack() as s2:
                        ST32 = sb("q_ST", [128, G * GW], F32, s2)
                        STb = sb("q_STb", [128, G * GW], BF16, s2)
                        nwrep = sb("q_nw", [128, DI], F32, s2)
                        S.op("dve", lambda e: e.memset(ST32[:], 0.0), writes=[("ST32", gi) for gi in range(G)])
                        S.op("pool", lambda e: e.memset(STb[:], 0.0), writes=[("STb", gi) for gi in range(G)])
                        S.dma(nwrep[:], ssd_nw_in[j:j + 1, :].to_broadcast([128, DI]), writes=["nwrep"])
                        bufs = {"dt": sb("r_dt", [128, NH], F32, s2), "da": sb("r_da", [128, NH], F32, s2),
                                "eacs": sb("r_ea", [128, NH], F32, s2), "eend": sb("r_ee", [128, NH], F32, s2),
                                "etot": sb("r_et", [128, NH], F32, s2)}
                        for nm, shp, dt_ in (("xs", GW, F32), ("zs", GW, F32), ("bt", 128, BF16), ("ct", 128, BF16),
                                             ("btm", 128, BF16), ("xdt", GW, BF16), ("xdtw", GW, BF16),
                                             ("cbm", 128, F32), ("R", HPG * 128, F32), ("dec", HQ * 128, F32),
                                             ("wT", HQ * 128, BF16), ("y", GW, F32), ("tmp", GW, F32), ("ss", 1, F32),
                                             ("rs", 1, F32), ("hn", GW, BF16), ("ho", GW, BF16)):
                            bufs[nm] = [sb("r_%s%d" % (nm, i), [128, shp], dt_, s2) for i in range(2)]
                        for ch in range(TOK // 128):
                            stage_ssd2(ch, j, ST32, STb, hp, nwrep, bufs)
                        barrier()
                for blk in range(NB):
                    stage_ssd3(blk, j)
                    stage_ln(blk, layer, 0, False)
            else:
                with ExitStack() as sl:
                    carry = sb("q_carry2", [128, KC * 2], F32, sl)
                    cw = sb("q_cw2", [128, KC * 3], F32, sl)
                    S.op("dve", lambda e: e.memset(carry[:], 0.0), writes=[("carry", fc) for fc in range(KC)])
                    S.dma(cw[:], sc_cw_in[j], writes=["sc_cw"])
                    barrier()
                    for blk in range(NB):
                        stage_sc(blk, j, carry, cw)
                        stage_ln(blk, layer, 0, False)
            with ExitStack() as sl:
                skT = sb("q_sk", [128, 16 * 128], F32, sl)
                S.dma(skT[:], sk_in[layer], writes=["skT"])
                barrier()
                for blk in range(NB):
                    stage_peer(blk, layer, skT)
                    stage_ln(blk, layer, 1, layer == c.DEPTH - 1)

    import os
    kstop = int(os.environ.get("KSTOP", "-1"))

    class _Stop(Exception):
        pass

    stage_ctr = [0]

    def _wrap(fn):
        def w(*a, **k):
            if kstop >= 0 and stage_ctr[0] >= kstop:
                return None
            stage_ctr[0] += 1
            return fn(*a, **k)
        return w

    stage_ssd1 = _wrap(stage_ssd1_)
    stage_ssd2 = _wrap(stage_ssd2_)
    stage_ssd3 = _wrap(stage_ssd3_)
    stage_ln = _wrap(stage_ln_)
    stage_sc = _wrap(stage_sc_)
    stage_peer = _wrap(stage_peer_)
    stage_init()
    if not os.environ.get('KSKIPFWD'):
        _forward()
    if kstop >= 0:
        barrier()
        with ExitStack() as sx:
            tb_ = sb("dbg_t", [128, D], F32, sx)
            for t in range(TOK // 128):
                S.dma(tb_[:], XR[t * 128:(t + 1) * 128, :], writes=[tb_.name])
                S.dma(y_out[t * 128:(t + 1) * 128, :], tb_[:], reads=[tb_.name], writes=[])
    for name in cast_done:
        S._wait("sp", ("ag_" + name, S.wcast[name]))
    barrier()
    es.close()
    return nc, [n for n, _ in wspecs]


def prepare_inputs(cfg, x, ssd_in_proj, ssd_conv_w, ssd_conv_b, ssd_dt_bias, ssd_A_log, ssd_D, ssd_norm_w,
                   ssd_out_proj, sc_in_proj, sc_conv_w, sc_out_proj, peer_wq, peer_subkeys,
                   peer_u, peer_v, ln1_g, ln1_b, ln2_g, ln2_b):
    c = cfg
    D, DI, NH, CONVD, KC = c.D, c.DI, c.NH, c.CONVD, c.KC
    f = lambda a: np.asarray(a, dtype=np.float32)
    blobs = {}
    n_ssd = (c.DEPTH + 1) // 2
    n_sc = c.DEPTH // 2
    for j in range(n_ssd):
        W = f(ssd_in_proj[j])
        blobs["ssd_wz%d" % j] = shard_blob(tm_tiles(W[:, :DI], 512))
        blobs["ssd_wx%d" % j] = shard_blob(fm_tiles(W[:, DI:DI + CONVD]))
        blobs["ssd_wdt%d" % j] = shard_blob(tm_tiles(W[:, DI + CONVD:], NH))
        blobs["ssd_wo%d" % j] = shard_blob(tm_tiles(f(ssd_out_proj[j]), 512))
    for j in range(n_sc):
        blobs["sc_wi%d" % j] = shard_blob(fm_tiles(f(sc_in_proj[j])))
        blobs["sc_wo%d" % j] = shard_blob(tm_tiles(f(sc_out_proj[j]), 512))
    for i in range(c.DEPTH):
        blobs["wq%d" % i] = shard_blob(fm_tiles(f(peer_wq[i])))
        blobs["ut%d" % i] = shard_blob(fm_tiles(f(peer_u[i]).T))
        blobs["v%d" % i] = shard_blob(tm_tiles(f(peer_v[i]), 512))
    common = {}
    eye = np.eye(128, dtype=np.float32)
    tri = np.triu(np.ones((128, 128), np.float32))
    U = 1.0 - tri
    common["consts"] = np.concatenate([eye, tri, U, np.ones((128, 128), np.float32)], axis=1)
    sk = f(peer_subkeys)
    common["skT"] = np.ascontiguousarray(sk.transpose(0, 4, 1, 2, 3).reshape(c.DEPTH, 128, 16 * 128))
    common["lnp"] = np.ascontiguousarray(np.stack([f(ln1_g), f(ln1_b), f(ln2_g), f(ln2_b)], axis=1))
    if n_ssd:
        cw = f(ssd_conv_w)
        cb = f(ssd_conv_b)
        allc = np.concatenate([cw, cb[:, None, :]], axis=1)
        allc = allc.reshape(n_ssd, 5, CONVD // 128, 128).transpose(0, 3, 2, 1)
        common["ssd_cw"] = np.ascontiguousarray(allc.reshape(n_ssd, 128, -1))
        common["ssd_hp"] = np.ascontiguousarray(np.stack([f(ssd_dt_bias), f(ssd_A_log), f(ssd_D)], axis=1))
        common["ssd_nw"] = f(ssd_norm_w)
    if n_sc:
        w = f(sc_conv_w).reshape(n_sc, 3, KC, 128).transpose(0, 3, 2, 1)
        common["sc_cw"] = np.ascontiguousarray(w.reshape(n_sc, 128, -1))
    xs = f(x)
    in_maps = []
    for core in range(NCORES):
        m = dict(common)
        m["x"] = np.ascontiguousarray(xs[core % c.BATCH])
        for k, v in blobs.items():
            m[k] = v
        in_maps.append(m)
    return in_maps


_CACHE = {}


def run(cfg, inputs):
    in_maps = prepare_inputs(cfg, **inputs)
    key = (cfg.D, cfg.SEQ, cfg.DEPTH, cfg.DI)
    if key not in _CACHE:
        _CACHE[key] = build_program(cfg)
    nc, _ = _CACHE[key]
    res = run_bass_kernel_spmd(nc, in_maps, core_ids=list(range(NCORES)))
    out = np.stack([np.asarray(res.results[b]["y"], dtype=np.float32) for b in range(cfg.BATCH)], axis=0)
    return out


def kernel(**inputs):
    return run(Cfg(), inputs)
```
